# Optimizing a Trainium2 kernel written in Bass

```python
import math
import jax, jax.numpy as jnp
from jax import lax
import numpy as np

D_MODEL = 2048
BATCH = 16
SEQ = 256
DEPTH = 2
DEC_BATCH = 4
DEC_SEQ = 1024
PAST_LEN = 512

GRID_W = 64
N_MIXERS = 4
BRANCH = D_MODEL // N_MIXERS
N_GROUPS = 4
GROUP_W = BRANCH // N_GROUPS
CHUNK = 128
POOL_WINDOWS = (2, 4, 8, 16)
N_HEADS_C = 4
HEAD_DIM_C = BRANCH // N_HEADS_C
QK_HALF = HEAD_DIM_C // 2
ROPE_AXIS_DIM = QK_HALF // 2
ROPE_BASE = 10000.0
Q_BLOCK = 128
SHORT_CONV = 3
HY_BANDS = 16
HY_EMB = 1 + 2 * HY_BANDS
HY_HIDDEN = 64
HY_ORDER = 2
HY_FAST_DECAY = 0.3
HY_SLOW_DECAY = 1.5
HY_TARGET = 1e-2
N_IN_PIECES = 13
D_IN = N_IN_PIECES * BRANCH
D_MIX = N_MIXERS * BRANCH
LN_EPS = 1e-6
F32 = jnp.float32

kernel_name = "hybrid_prefix_diffusion_step"


def _layer_norm(x):
    x32 = x.astype(F32)
    mu = jnp.mean(x32, axis=-1, keepdims=True)
    xc = x32 - mu
    var = jnp.mean(xc * xc, axis=-1, keepdims=True)
    return (xc * lax.rsqrt(var + LN_EPS)).astype(x.dtype)


def _chunk_gmlp(u, v, w_s, b_s):
    b, L, _ = v.shape
    vc = _layer_norm(v).reshape(b, L // CHUNK, CHUNK, N_GROUPS, GROUP_W)
    mixed = jnp.einsum("gij,bnjgc->bnigc", w_s, vc) + jnp.swapaxes(b_s, 0, 1)[:, :, None]
    return u * mixed.reshape(b, L, BRANCH)


def _multiscale_pool(p, w_pool, pool_scale):
    b, L, _ = p.shape
    pg = p.reshape(b, L, N_GROUPS, GROUP_W)
    csum = jnp.pad(jnp.cumsum(pg.astype(F32), axis=1), ((0, 0), (1, 0), (0, 0), (0, 0)))
    t = jnp.arange(L)
    means = []
    for g, w in enumerate(POOL_WINDOWS):
        lo = jnp.clip(t - w // 2, 0, L)
        hi = jnp.clip(t + w // 2, 0, L)
        win = csum[:, hi, g] - csum[:, lo, g]
        means.append(win / (hi - lo).astype(F32)[None, :, None])
    pooled = jnp.stack(means, axis=2).astype(p.dtype)
    y = jnp.einsum("blgc,gcd->blgd", pooled - pg, w_pool)
    return y.reshape(b, L, BRANCH) * pool_scale


def _axial_angles(L):
    n_rows = L // GRID_W
    row = jnp.repeat(jnp.arange(n_rows), GRID_W).astype(F32)
    col = jnp.tile(jnp.arange(GRID_W), n_rows).astype(F32)
    inv = ROPE_BASE ** (-jnp.arange(0, ROPE_AXIS_DIM, 2, dtype=F32) / ROPE_AXIS_DIM)
    return row[:, None] * inv, col[:, None] * inv


def _rotate(x, ang):
    half = x.shape[-1] // 2
    cos = jnp.cos(ang)[None, :, None, None, :].astype(x.dtype)
    sin = jnp.sin(ang)[None, :, None, None, :].astype(x.dtype)
    x1, x2 = x[..., :half], x[..., half:]
    return jnp.concatenate([x1 * cos - x2 * sin, x2 * cos + x1 * sin], axis=-1)


def _axial_rope(x, ang_row, ang_col):
    return jnp.concatenate([_rotate(x[..., :ROPE_AXIS_DIM], ang_row),
                            _rotate(x[..., ROPE_AXIS_DIM:], ang_col)], axis=-1)


def _diff_attention(q, k, v, lam, lam_init, subln_w):
    bsz, lq = q.shape[0], q.shape[1]
    n_blk = lq // Q_BLOCK
    qb = jnp.moveaxis(q.reshape(bsz, n_blk, Q_BLOCK, N_HEADS_C, 2, QK_HALF), 1, 0)

    def block(q_blk):
        s = jnp.einsum("bqhmd,bkhmd->bhmqk", q_blk, k, preferred_element_type=F32) * (QK_HALF ** -0.5)
        p = jax.nn.softmax(s, axis=-1)
        w = (p[:, :, 0] - lam * p[:, :, 1]).astype(v.dtype)
        return jnp.einsum("bhqk,bkhd->bqhd", w, v)

    o = jnp.moveaxis(lax.map(block, qb), 0, 1).reshape(bsz, lq, N_HEADS_C, HEAD_DIM_C)
    o32 = o.astype(F32)
    o32 = o32 * lax.rsqrt(jnp.mean(o32 * o32, axis=-1, keepdims=True) + 1e-5)
    o = o32.astype(v.dtype) * subln_w * (1.0 - lam_init)
    return o.reshape(bsz, lq, BRANCH)


def _short_conv(x, w, b):
    L = x.shape[1]
    xp = jnp.pad(x, ((0, 0), (1, 1), (0, 0)))
    return xp[:, 0:L] * w[0] + xp[:, 1:L + 1] * w[1] + xp[:, 2:L + 2] * w[2] + b


def _hyena_filters(L, w1, b1, w2, b2, freq, w3):
    t_idx = jnp.arange(L, dtype=F32)
    t_norm = jnp.linspace(0.0, 1.0, L, dtype=F32)
    bands = jnp.linspace(1e-4, HY_BANDS - 1, HY_BANDS, dtype=F32)
    ang = (2.0 * math.pi * t_idx / L)[:, None] * bands[None, :]
    feats = jnp.concatenate([t_norm[:, None], jnp.cos(ang), jnp.sin(ang)], axis=-1)
    fr = freq.astype(F32)
    h = jnp.sin(fr * (feats @ w1.astype(F32) + b1.astype(F32)))
    h = jnp.sin(fr * (h @ w2.astype(F32) + b2.astype(F32)))
    h = (h @ w3.astype(F32)).reshape(L, HY_ORDER, 2, BRANCH)
    deltas = jnp.abs(jnp.linspace(math.log(HY_TARGET) / HY_FAST_DECAY,
                                  math.log(HY_TARGET) / HY_SLOW_DECAY, BRANCH, dtype=F32))
    h = h * jnp.exp(-t_norm[:, None] * deltas[None, :])[:, None, None, :]
    fwd, bwd = h[:, :, 0], h[:, :, 1]
    k = jnp.concatenate([fwd, jnp.zeros_like(fwd[:1]), bwd[1:][::-1]], axis=0)
    k = k / jnp.sum(jnp.abs(k), axis=0, keepdims=True)
    return jnp.fft.rfft(k, axis=0)


def _long_conv(z, k_f, bias):
    L = z.shape[1]
    z32 = z.astype(F32)
    y = jnp.fft.irfft(jnp.fft.rfft(z32, n=2 * L, axis=1) * k_f[None], n=2 * L, axis=1)[:, :L]
    return (y + z32 * bias.astype(F32)).astype(z.dtype)


def _hyena(x1, x2, hv, conv_w, conv_b, k_f, hy_bias):
    xc = _short_conv(jnp.concatenate([x1, x2, hv], axis=-1), conv_w, conv_b)
    g1, g2, z = jnp.split(xc, 3, axis=-1)
    z = g1 * _long_conv(z, k_f[:, 0], hy_bias[0])
    return g2 * _long_conv(z, k_f[:, 1], hy_bias[1])


def _trunk_layer(x, cond, ctx_k, ctx_v, layer_idx, lp):
    bsz, L, _ = x.shape
    alpha = (2.0 * DEPTH) ** 0.25
    mod = jax.nn.silu(cond) @ lp["w_mod"] + lp["b_mod"]
    shift, scale, gate = jnp.split(mod[:, None, :], 3, axis=-1)
    h = _layer_norm(x) * (1.0 + scale) + shift
    (a_u, a_v, a_g, b_x, b_g, c_q, c_k, c_v, c_g,
     d_x1, d_x2, d_v, d_g) = jnp.split(h @ lp["w_in"], N_IN_PIECES, axis=-1)

    y_a = jax.nn.silu(a_g) * _chunk_gmlp(a_u, a_v, lp["gmlp_w"], lp["gmlp_b"])
    y_b = jax.nn.silu(b_g) * _multiscale_pool(b_x, lp["pool_w"], lp["pool_scale"])

    q = c_q.reshape(bsz, L, N_HEADS_C, 2, QK_HALF)
    k = c_k.reshape(bsz, L, N_HEADS_C, 2, QK_HALF)
    v = c_v.reshape(bsz, L, N_HEADS_C, HEAD_DIM_C)
    if ctx_k is None:
        q_att, keys, vals = q, k, v
    else:
        ang_r, ang_c = _axial_angles(L)
        q_att = _axial_rope(q, ang_r, ang_c)
        keys = jnp.concatenate([ctx_k, _axial_rope(k, ang_r, ang_c)], axis=1)
        vals = jnp.concatenate([ctx_v, v], axis=1)
    lam_qk = lp["lambda_qk"].astype(F32)
    lam_init = 0.8 - 0.6 * math.exp(-0.3 * layer_idx)
    lam = jnp.exp(jnp.sum(lam_qk[0] * lam_qk[1])) - jnp.exp(jnp.sum(lam_qk[2] * lam_qk[3])) + lam_init
    y_c = jax.nn.silu(c_g) * _diff_attention(q_att, keys, vals, lam, lam_init, lp["subln_w"])

    k_f = _hyena_filters(L, lp["filt_w1"], lp["filt_b1"], lp["filt_w2"], lp["filt_b2"],
                         lp["filt_freq"], lp["filt_w3"])
    y_d = jax.nn.silu(d_g) * _hyena(d_x1, d_x2, d_v, lp["conv_w"], lp["conv_b"], k_f, lp["hyena_bias"])

    y = jnp.concatenate([y_a, y_b, y_c, y_d], axis=-1) @ lp["w_out"] + lp["b_out"]
    x = _layer_norm(alpha * x + gate * y) * lp["ln_g"] + lp["ln_b"]
    return x, k, v


def setup_inputs(seed: int = 0) -> dict:
    key = jax.random.key(seed)
    ks = jax.random.split(key, 28)

    def nrm(i, shape, s):
        return jax.random.normal(ks[i], shape, F32) * s

    beta = (8.0 * DEPTH) ** -0.25
    return {
        "x_prompt": nrm(0, (BATCH, SEQ, D_MODEL), 1.0),
        "x_sample": nrm(1, (DEC_BATCH, DEC_SEQ, D_MODEL), 1.0),
        "cache_k": nrm(2, (DEC_BATCH, DEPTH, PAST_LEN, N_HEADS_C, 2, QK_HALF), 1.0),
        "cache_v": nrm(3, (DEC_BATCH, DEPTH, PAST_LEN, N_HEADS_C, HEAD_DIM_C), 1.0),
        "c": nrm(4, (DEC_BATCH, D_MODEL), 1.0),
        "c_ctx": nrm(5, (D_MODEL,), 1.0),
        "w_mod": nrm(6, (DEPTH, D_MODEL, 3 * D_MODEL), 0.5 * D_MODEL ** -0.5),
        "b_mod": nrm(7, (DEPTH, 3 * D_MODEL), 0.01),
        "w_in": nrm(8, (DEPTH, D_MODEL, D_IN), D_MODEL ** -0.5),
        "gmlp_w": nrm(9, (DEPTH, N_GROUPS, CHUNK, CHUNK), CHUNK ** -0.5),
        "gmlp_b": 1.0 + nrm(10, (DEPTH, N_GROUPS, CHUNK), 0.01),
        "pool_w": nrm(11, (DEPTH, N_GROUPS, GROUP_W, GROUP_W), GROUP_W ** -0.5),
        "pool_scale": 1.0 + nrm(12, (DEPTH, BRANCH), 0.02),
        "lambda_qk": nrm(13, (DEPTH, 4, QK_HALF), 0.1),
        "subln_w": 1.0 + nrm(14, (DEPTH, HEAD_DIM_C), 0.02),
        "conv_w": nrm(15, (DEPTH, SHORT_CONV, 3 * BRANCH), SHORT_CONV ** -0.5),
        "conv_b": nrm(16, (DEPTH, 3 * BRANCH), 0.01),
        "filt_w1": nrm(17, (DEPTH, HY_EMB, HY_HIDDEN), HY_EMB ** -0.5),
        "filt_b1": nrm(18, (DEPTH, HY_HIDDEN), 0.01),
        "filt_w2": nrm(19, (DEPTH, HY_HIDDEN, HY_HIDDEN), HY_HIDDEN ** -0.5),
        "filt_b2": nrm(20, (DEPTH, HY_HIDDEN), 0.01),
        "filt_freq": 1.0 + nrm(21, (DEPTH, HY_HIDDEN), 0.02),
        "filt_w3": nrm(22, (DEPTH, HY_HIDDEN, HY_ORDER * 2 * BRANCH), HY_HIDDEN ** -0.5),
        "hyena_bias": nrm(23, (DEPTH, HY_ORDER, BRANCH), 0.1),
        "w_out": nrm(24, (DEPTH, D_MIX, D_MODEL), beta * D_MIX ** -0.5),
        "b_out": nrm(25, (DEPTH, D_MODEL), 0.01),
        "ln_g": 1.0 + nrm(26, (DEPTH, D_MODEL), 0.02),
        "ln_b": nrm(27, (DEPTH, D_MODEL), 0.01),
    }


def reference(x_prompt, x_sample, cache_k, cache_v, c, c_ctx, w_mod, b_mod, w_in, gmlp_w, gmlp_b,
              pool_w, pool_scale, lambda_qk, subln_w, conv_w, conv_b, filt_w1, filt_b1, filt_w2,
              filt_b2, filt_freq, filt_w3, hyena_bias, w_out, b_out, ln_g, ln_b):
    xp = x_prompt
    xs = x_sample
    cond_ctx = c_ctx[None, :]
    new_k = []
    new_v = []
    for l in range(DEPTH):
        lp = {
            "w_mod": w_mod[l], "b_mod": b_mod[l], "w_in": w_in[l],
            "gmlp_w": gmlp_w[l], "gmlp_b": gmlp_b[l],
            "pool_w": pool_w[l], "pool_scale": pool_scale[l],
            "lambda_qk": lambda_qk[l], "subln_w": subln_w[l],
            "conv_w": conv_w[l], "conv_b": conv_b[l],
            "filt_w1": filt_w1[l], "filt_b1": filt_b1[l], "filt_w2": filt_w2[l],
            "filt_b2": filt_b2[l], "filt_freq": filt_freq[l], "filt_w3": filt_w3[l],
            "hyena_bias": hyena_bias[l],
            "w_out": w_out[l], "b_out": b_out[l], "ln_g": ln_g[l], "ln_b": ln_b[l],
        }
        xp, k_l, v_l = _trunk_layer(xp, cond_ctx, None, None, l, lp)
        new_k.append(k_l)
        new_v.append(v_l)
        xs, _, _ = _trunk_layer(xs, c, cache_k[:, l], cache_v[:, l], l, lp)
    new_cache_k = jnp.stack(new_k, axis=1)
    new_cache_v = jnp.stack(new_v, axis=1)
    return (xp, xs, new_cache_k, new_cache_v)
```

```python
import contextlib
import math
import numpy as np
import ml_dtypes
import concourse.bass as bass
import concourse.mybir as mybir
from concourse.bass_utils import run_bass_kernel_spmd

F32 = mybir.dt.float32
BF16 = mybir.dt.bfloat16
ESZ = {F32: 4, BF16: 2}
AF = mybir.ActivationFunctionType
ALU = mybir.AluOpType
AX = mybir.AxisListType

DEPTH = 2
NT = 1024
DM = 2048
KC = 16
NPIECE = 13
LN_EPS = 1e-6
ALPHA = (2.0 * DEPTH) ** 0.25
TWO_PI = 2.0 * math.pi


class Acc:
    __slots__ = ("ap", "space", "lo", "hi", "p0", "p1")

    def __init__(self, ap, space, lo, hi, p0, p1):
        self.ap, self.space, self.lo, self.hi, self.p0, self.p1 = ap, space, lo, hi, p0, p1


class T:
    def __init__(self, arena_ap, space, off, dtype, shape, p0=0, pn=128):
        self.space, self.off, self.dtype, self.shape = space, off, dtype, list(shape)
        self.p0, self.pn = p0, pn
        es = ESZ[dtype]
        n = int(np.prod(shape))
        nbytes = n * es
        assert off % 4 == 0 and nbytes % 4 == 0, (off, nbytes)
        ap = arena_ap[p0:p0 + pn, off // 4:(off + nbytes) // 4]
        if dtype != F32:
            ap = ap.bitcast(dtype)
        if len(shape) == 2:
            ap = ap.rearrange("p (a b) -> p a b", b=shape[1])
        elif len(shape) == 3:
            ap = ap.rearrange("p (a b c) -> p a b c", b=shape[1], c=shape[2])
        self.ap = ap
        self.es = es
        st = []
        acc = 1
        for d in reversed(self.shape):
            st.append(acc)
            acc *= d
        self.strides = list(reversed(st))
        self.nbytes = nbytes

    def __getitem__(self, idx):
        if not isinstance(idx, tuple):
            idx = (idx,)
        pidx = idx[0]
        fidx = list(idx[1:]) + [slice(None)] * (len(self.shape) - len(idx) + 1)
        if isinstance(pidx, slice):
            ps = 0 if pidx.start is None else pidx.start
            pe = self.pn if pidx.stop is None else pidx.stop
        else:
            ps, pe = pidx, pidx + 1
        pidx = slice(ps, pe)
        lo = 0
        hi = 0
        for k, ix in enumerate(fidx):
            d = self.shape[k]
            if isinstance(ix, slice):
                s = 0 if ix.start is None else ix.start
                e = d if ix.stop is None else ix.stop
                last = e - 1
            else:
                s, last = ix, ix
            assert 0 <= s <= last < d, (self.shape, idx)
            lo += s * self.strides[k]
            hi += last * self.strides[k]
        ap = self.ap[(pidx,) + tuple(fidx)]
        blo, bhi = self.off + lo * self.es, self.off + (hi + 1) * self.es
        if self.space == "ps":
            blo = (blo // 2048) * 2048
            bhi = ((bhi + 2047) // 2048) * 2048
            return Acc(ap, self.space, blo, bhi, 0, 128)
        return Acc(ap, self.space, blo, bhi, self.p0 + ps, self.p0 + pe)

    def all(self):
        return self[tuple([slice(None)] * (len(self.shape) + 1))]


def dacc(ap, name, lo=0, hi=1 << 40):
    return Acc(ap, "dram:" + name, lo, hi, 0, 1)


class Op:
    __slots__ = ("eng", "fn", "deps", "signal", "sigval", "dma_sem", "dma_val", "tag", "waits")


class Sched:
    ENGS = ("pe", "act", "dve", "pool", "sp")

    def __init__(self):
        self.ops = {e: [] for e in self.ENGS}
        self.recs = {}
        self.dma_counts = {}
        self.group_sems = set()
        self.all_ops = []
        self._fin = False

    @staticmethod
    def _ov(r, a):
        return r[0] < a.hi and a.lo < r[1] and r[2] < a.p1 and a.p0 < r[3]

    def add(self, eng, fn, reads=(), writes=(), dma_sem=None):
        op = Op()
        op.eng, op.fn, op.signal, op.sigval = eng, fn, False, 0
        op.dma_sem, op.dma_val = dma_sem, 0
        op.tag = getattr(self, "phase", "")
        deps = []
        for a in reads:
            for r in self.recs.setdefault(a.space, []):
                if self._ov(r, a):
                    if r[4] is not None:
                        deps.append(r[4])
                    if a.space == "ps":
                        deps.extend(x for x in r[5] if x.eng != eng)
                    r[5].append(op)
        for a in writes:
            recs = self.recs.setdefault(a.space, [])
            keep = []
            for r in recs:
                if self._ov(r, a):
                    if r[4] is not None:
                        deps.append(r[4])
                    deps.extend(r[5])
                    covered = (a.lo <= r[0] and r[1] <= a.hi and a.p0 <= r[2] and r[3] <= a.p1)
                    if not covered:
                        keep.append(r)
                else:
                    keep.append(r)
            keep.append([a.lo, a.hi, a.p0, a.p1, op, []])
            self.recs[a.space] = keep
        seen = set()
        od = []
        for d in deps:
            if id(d) in seen or d is op:
                continue
            seen.add(id(d))
            if d.dma_sem is None and d.eng == eng and eng in ("pe", "sp"):
                continue
            od.append(d)
            if d.dma_sem is None:
                d.signal = True
        op.deps = od
        if dma_sem is not None:
            c = self.dma_counts.get(dma_sem, 0) + 1
            self.dma_counts[dma_sem] = c
            op.dma_val = 16 * c
        self.ops[eng].append(op)
        self.all_ops.append(op)
        return op

    def _plan_waits(self):
        rcs = {e: {} for e in self.ENGS}
        clock_at = {}
        for op in self.all_ops:
            rc = rcs[op.eng]
            waits = {}
            for d in op.deps:
                if d.dma_sem is not None:
                    k, v = ("dma", d.dma_sem), d.dma_val
                    if d.dma_sem in self.group_sems:
                        v = 16 * self.dma_counts[d.dma_sem]
                else:
                    k, v = ("eng", d.eng), d.sigval
                if rc.get(k, 0) >= v:
                    continue
                if waits.get(k, 0) < v:
                    waits[k] = v
                rc[k] = v
                for kk, vv in clock_at[id(d)].items():
                    if rc.get(kk, 0) < vv:
                        rc[kk] = vv
            op.waits = list(waits.items())
            snap = dict(rc)
            if op.dma_sem is not None:
                snap[("dma", op.dma_sem)] = max(snap.get(("dma", op.dma_sem), 0), op.dma_val)
            elif op.signal:
                snap[("eng", op.eng)] = max(snap.get(("eng", op.eng), 0), op.sigval)
            clock_at[id(op)] = snap

    def emit_one(self, e, eng, sems, dma_sems, final=False):
        if not self._fin:
            for ee in self.ENGS:
                c = 0
                for op in self.ops[ee]:
                    if op.signal:
                        c += 1
                        op.sigval = c
            self._fin = True
        if not getattr(self, "_planned", False):
            self._plan_waits()
            self._planned = True
        for op in self.ops[e]:
            for k, v in op.waits:
                s = dma_sems[k[1]] if k[0] == "dma" else sems[k[1]]
                eng.wait_ge(s, v)
            ins = op.fn(eng)
            if op.dma_sem is not None:
                ins.then_inc(dma_sems[op.dma_sem], 16)
            elif op.signal:
                ins.then_inc(sems[e], 1)
        if final:
            for name, c in self.dma_counts.items():
                eng.wait_ge(dma_sems[name], 16 * c)


def _bf(a):
    return np.ascontiguousarray(a.astype(ml_dtypes.bfloat16))


def _tables(kind):
    L = 1024 if kind == "s" else 256
    nseg_seq = NT // L
    t = np.arange(NT)
    tl = t % L
    seg = t // L
    tb = {}
    p = np.arange(128)
    d = p % 64
    j = d % 16
    inv = 10000.0 ** (-(2.0 * j) / 32.0)
    if kind == "s":
        pos = np.where((d // 32 == 0)[:, None], (t // 64)[None, :], (t % 64)[None, :]).astype(np.float64)
        ang = pos * inv[:, None]
        cosT, sinT = np.cos(ang), np.sin(ang)
    else:
        cosT, sinT = np.ones((128, NT)), np.zeros((128, NT))
    tb["cos"] = cosT.astype(np.float32)
    tb["sin"] = sinT.astype(np.float32)
    mk = np.zeros((128, 48), np.float32)
    if kind == "p":
        for kt in range(12):
            for sg in range(4):
                ok = kt >= 4 and (kt - 4) // 2 == sg
                mk[:, kt * 4 + sg] = 0.0 if ok else -30000.0
    tb["maskb"] = mk
    rc = np.zeros((4, 1104), np.float32)
    for g, w in enumerate((2, 4, 8, 16)):
        lo = np.clip(tl - w // 2, 0, L)
        hi = np.clip(tl + w // 2, 0, L)
        cnt = (hi - lo).astype(np.float32)
        for s in range(4):
            rc[g, 16 + 272 * s:16 + 272 * s + 256] = 1.0 / cnt[256 * s:256 * s + 256]
    tb["rcpad"] = np.ascontiguousarray(np.broadcast_to(rc[:, None, :], (4, 128, 1104)))
    tn = np.linspace(0.0, 1.0, L, dtype=np.float32)
    bands = np.linspace(1e-4, 15.0, 16, dtype=np.float32)
    ang = (np.float32(2.0 * math.pi) * np.arange(L, dtype=np.float32) / np.float32(L))[:, None] * bands[None, :]
    feats = np.concatenate([tn[:, None], np.cos(ang), np.sin(ang)], axis=-1).astype(np.float32)
    feats = np.tile(feats, (nseg_seq, 1))
    ft = np.zeros((64, NT), np.float32)
    ft[:33] = feats.T
    tb["featsT"] = ft
    deltas = np.abs(np.linspace(math.log(1e-2) / 0.3, math.log(1e-2) / 1.5, 512, dtype=np.float32))
    dec = np.exp(-tn[:, None] * deltas[None, :]).astype(np.float32)
    tb["dec"] = np.ascontiguousarray(np.tile(dec, (nseg_seq, 1)))
    bm = (tl != 0).astype(np.float32)
    ms = (t < L).astype(np.float32)
    sm = np.zeros((128, 24), np.float32)
    sm[:, 0:8] = ms.reshape(8, 128).T
    sm[:, 8:16] = (ms * bm).reshape(8, 128).T
    sm[:, 16:24] = bm.reshape(8, 128).T
    tb["smask"] = sm
    f = np.arange(NT)
    fl = f % L
    om = math.pi * (fl + 0.5) / L
    same = (seg[:, None] == (f // L)[None, :])
    ph = tl[:, None].astype(np.float64) * om[None, :]
    C = np.where(same, np.cos(ph), 0.0)
    Sn = np.where(same, np.sin(ph), 0.0)
    wf = np.zeros((2, 128, 8, 1024), np.float32)
    for s in range(2):
        for pl in range(4):
            fr = (4 * s + pl) * 128
            wf[s, :, :, pl * 256:pl * 256 + 128] = C[:, fr:fr + 128].reshape(8, 128, 128).transpose(1, 0, 2)
            wf[s, :, :, pl * 256 + 128:pl * 256 + 256] = Sn[:, fr:fr + 128].reshape(8, 128, 128).transpose(1, 0, 2)
    tb["wf"] = _bf(wf)
    G = np.concatenate([C.T, Sn.T], axis=0) / L
    gi = np.zeros((2, 128, 16, 512), np.float32)
    for h in range(2):
        gi[h] = G[:, 512 * h:512 * h + 512].reshape(16, 128, 512).transpose(1, 0, 2)
    tb["gi"] = _bf(gi)
    tb["mint"] = np.full((128, 1), 1.0 if kind == "s" else 0.0, np.float32)
    return tb


def _fixed_consts():
    c = {}
    c["ident"] = np.eye(128, dtype=np.float32)
    c["ones"] = np.ones((128, 128), np.float32)
    p = np.arange(128)
    d = p % 64
    R = np.zeros((128, 128), np.float32)
    for pp in range(128):
        if (d[pp] % 32) < 16:
            R[pp, pp + 16] = -1.0
        else:
            R[pp, pp - 16] = 1.0
    c["ropeT"] = np.ascontiguousarray(R.T)
    l0 = np.zeros((128, 128), np.float32)
    l0[:64] = 1.0
    l1 = np.zeros((128, 128), np.float32)
    l1[64:] = 1.0
    c["lsel"] = np.ascontiguousarray(np.stack([l0, l1], axis=1))
    return c


PP = {}
_o = 0
for _n, _w in (("bout", 16), ("lng", 16), ("lnb", 16), ("psc", 4), ("subln", 1), ("convw", 36), ("convb", 12),
               ("gbs", 512), ("hb", 1024), ("lq", 256), ("fb1", 1), ("fb2", 1), ("ffr", 1), ("fw1", 64), ("fw2", 64)):
    PP[_n] = (_o, _w)
    _o += _w
NPP = _o


def _pack_params(inp, l):
    pp = np.zeros((128, NPP), np.float32)

    def put(name, arr):
        o, w = PP[name]
        pp[:arr.shape[0], o:o + w] = arr
    put("bout", inp["b_out"][l].reshape(16, 128).T)
    put("lng", inp["ln_g"][l].reshape(16, 128).T)
    put("lnb", inp["ln_b"][l].reshape(16, 128).T)
    put("psc", inp["pool_scale"][l].reshape(4, 128).T)
    put("subln", inp["subln_w"][l].reshape(128, 1))
    put("convw", inp["conv_w"][l].reshape(3, 12, 128).transpose(2, 0, 1).reshape(128, 36))
    put("convb", inp["conv_b"][l].reshape(12, 128).T)
    put("gbs", np.broadcast_to(inp["gmlp_b"][l].reshape(1, 512), (128, 512)))
    put("hb", np.broadcast_to(inp["hyena_bias"][l].reshape(1, 1024), (128, 1024)))
    put("lq", np.broadcast_to(inp["lambda_qk"][l].reshape(1, 256), (128, 256)))
    put("fb1", inp["filt_b1"][l].reshape(64, 1))
    put("fb2", inp["filt_b2"][l].reshape(64, 1))
    put("ffr", inp["filt_freq"][l].reshape(64, 1))
    put("fw1", inp["filt_w1"][l])
    put("fw2", inp["filt_w2"][l])
    return pp


class _Stop(Exception):
    pass


def build_program(debug=False, nlayers=DEPTH, limit=None, plan="auto"):
    if plan == "auto":
        plan = build_program(debug=debug, nlayers=nlayers, limit=limit, plan=None)._ring_plan
    return _build_program(debug, nlayers, limit, plan)


def _build_program(debug, nlayers, limit, plan):
    nc = bass.Bass("TRN2", target_bir_lowering=False)

    def din(name, shape, dt=F32):
        return nc.dram_tensor(name, list(shape), dt, kind="ExternalInput").ap()

    def dout(name, shape, dt=F32):
        return nc.dram_tensor(name, list(shape), dt, kind="ExternalOutput").ap()

    xT_d = din("xT", [DM, NT])
    cond_d = din("cond", [128, 16])
    ckT_d = din("ckT", [DEPTH, 512, 512])
    cv_d = din("cv", [DEPTH, 512, 512])
    wmod_d = din("w_mod", [DEPTH, DM, 3 * DM])
    bmod_d = din("b_modT", [DEPTH, 128, 48])
    win_d = din("w_in", [DEPTH, DM, NPIECE * 512])
    wout_d = din("w_out", [DEPTH, DM, DM])
    gw_d = din("gmlp_wT", [DEPTH, 128, 4, 128])
    pw_d = din("pool_w", [DEPTH, 128, 4, 128])
    w3_d = din("filt_w3", [DEPTH, 64, 2048])
    pp_d = din("pp", [DEPTH, 128, NPP])
    cos_d = din("cos", [128, NT])
    sin_d = din("sin", [128, NT])
    maskb_d = din("maskb", [128, 48])
    rcpad_d = din("rcpad", [4, 128, 1104])
    featsT_d = din("featsT", [64, NT])
    dec_d = din("dec", [NT, 512])
    smask_d = din("smask", [128, 24])
    wf_d = din("wf", [2, 128, 8, 1024], BF16)
    gi_d = din("gi", [2, 128, 16, 512], BF16)
    mint_d = din("mint", [128, 1])
    ident_d = din("ident", [128, 128])
    ones_d = din("ones", [128, 128])
    ropeT_d = din("ropeT", [128, 128])
    lsel_d = din("lsel", [128, 2, 128])

    yT_o = dout("yT_out", [DM, NT])
    kT_o = dout("kT_out", [DEPTH, 512, NT])
    v_o = dout("v_out", [DEPTH, NT, 512])
    xs_d = nc.dram_tensor("xspill", [DM, NT], F32, kind="Internal").ap()
    kp_d = nc.dram_tensor("kspec", [2, 8, 128, 1024], F32, kind="Internal").ap()
    dbg = {}
    if debug:
        dbg["hT"] = dout("dbg_hT", [128, 16 * NT], BF16)
        dbg["yT"] = dout("dbg_yT", [128, 16 * NT], BF16)
        dbg["mod"] = dout("dbg_mod", [128, 48])

    es = contextlib.ExitStack()
    with es:
        ARENA = 212000
        sb = es.enter_context(nc.sbuf_tensor("arena", [128, ARENA // 4], F32))
        ps = es.enter_context(nc.psum_tensor("psarena", [128, 4096], F32))
        sems = {e: es.enter_context(nc.semaphore("s_" + e)) for e in Sched.ENGS}
        dsems = {}
        S = Sched()
        sba = sb[:, :]
        psa = ps[:, :]

        cur = [0]

        def alloc(dtype, shape, pn=128):
            t = T(sba, "sb", cur[0], dtype, shape, 0, pn)
            cur[0] += (t.nbytes + 31) // 32 * 32
            return t

        def at(off, dtype, shape, pn=128):
            return T(sba, "sb", off, dtype, shape, 0, pn)

        banks = [T(psa, "ps", 2048 * i, F32, [512]) for i in range(8)]
        banks_bf = [T(psa, "ps", 2048 * i, BF16, [1024]) for i in range(8)]
        bctr = [0]

        reserved = set()

        def nbank():
            b = bctr[0] % 8
            bctr[0] += 1
            while b in reserved:
                b = bctr[0] % 8
                bctr[0] += 1
            return b

        def EN(e, name):
            return e

        POOL_TO_DVE = True

        def TT(eng, out, in0, in1, op):
            if POOL_TO_DVE and eng == "pool":
                eng = "dve"
            S.add(eng, lambda e: e.tensor_tensor(out=out.ap, in0=in0.ap, in1=in1.ap, op=op),
                  reads=[in0, in1], writes=[out])

        def TS(eng, out, in0, s1, s2, op0, op1=None):
            if POOL_TO_DVE and eng == "pool":
                eng = "dve"
            rd = [in0]
            a1 = s1.ap if isinstance(s1, Acc) else s1
            a2 = s2.ap if isinstance(s2, Acc) else s2
            if isinstance(s1, Acc):
                rd.append(s1)
            if isinstance(s2, Acc):
                rd.append(s2)
            if op1 is None:
                S.add(eng, lambda e: e.tensor_scalar(out=out.ap, in0=in0.ap, scalar1=a1, scalar2=None, op0=op0),
                      reads=rd, writes=[out])
            else:
                S.add(eng, lambda e: e.tensor_scalar(out=out.ap, in0=in0.ap, scalar1=a1, scalar2=a2, op0=op0, op1=op1),
                      reads=rd, writes=[out])

        def STT(eng, out, in0, sc, in1, op0, op1):
            eng = "dve"
            rd = [in0, in1]
            a = sc.ap if isinstance(sc, Acc) else sc
            if isinstance(sc, Acc):
                rd.append(sc)
            S.add(eng, lambda e: e.scalar_tensor_tensor(out=out.ap, in0=in0.ap, scalar=a, in1=in1.ap, op0=op0, op1=op1),
                  reads=rd, writes=[out])

        def ACT(out, in_, func, bias=0.0, scale=1.0):
            rd = [in_]
            b = bias.ap if isinstance(bias, Acc) else bias
            s = scale.ap if isinstance(scale, Acc) else scale
            if isinstance(bias, Acc):
                rd.append(bias)
            if isinstance(scale, Acc):
                rd.append(scale)
            S.add("act", lambda e: e.activation(out=out.ap, in_=in_.ap, func=func, bias=b, scale=s),
                  reads=rd, writes=[out])

        def CP(eng, out, in_):
            if eng == "act":
                S.add("act", lambda e: e.copy(out=out.ap, in_=in_.ap), reads=[in_], writes=[out])
            else:
                S.add(eng, lambda e: e.tensor_copy(out=out.ap, in_=in_.ap), reads=[in_], writes=[out])

        def MSET(eng, out, val):
            S.add(eng, lambda e: e.memset(out.ap, val), writes=[out])

        def MM(out, lhsT, rhs, start, stop):
            S.add("pe", lambda e: e.matmul(out.ap, lhsT=lhsT.ap, rhs=rhs.ap, start=start, stop=stop),
                  reads=[lhsT, rhs], writes=[out])

        def TR(out, in_, ident):
            S.add("pe", lambda e: e.transpose(out=out.ap, in_=in_.ap, identity=ident.ap),
                  reads=[in_, ident], writes=[out])

        def DMA(q, out, in_, sem, reads, writes):
            oa = out.ap if isinstance(out, Acc) else out
            ia = in_.ap if isinstance(in_, Acc) else in_
            S.add(q, lambda e: e.dma_start(out=oa, in_=ia), reads=reads, writes=writes, dma_sem=sem)

        def RECIP(eng, out, in_):
            S.add(eng, lambda e: e.reciprocal(out=out.ap, in_=in_.ap), reads=[in_], writes=[out])

        IDF = alloc(F32, [128])
        IDB = alloc(BF16, [128])
        ONF = alloc(F32, [128])
        ONB = alloc(BF16, [128])
        LSEL = alloc(BF16, [2, 128])
        LSELF = alloc(F32, [2, 128])
        ROPT = alloc(F32, [128])
        COS = alloc(F32, [NT])
        SIN = alloc(F32, [NT])
        MASKB = alloc(F32, [48])
        MINT = alloc(F32, [1])
        SMASK = alloc(F32, [24])
        CONDT = alloc(F32, [16])
        SCT = alloc(BF16, [16])
        MODT = alloc(F32, [48])
        S1T = alloc(F32, [16])
        GBT = alloc(F32, [16])
        PPS = alloc(F32, [NPP])
        WST = alloc(BF16, [4, 128])
        WPB = alloc(BF16, [4, 128])
        W3B = alloc(BF16, [2048], 64)
        FEAT = alloc(F32, [NT], 64)
        SMALL = alloc(F32, [128])
        BT = alloc(F32, [8, 48])
        STG = alloc(F32, [4, 512])
        RING = [alloc(BF16, [16, 512]), alloc(BF16, [16, 512])]
        HT = alloc(BF16, [16, NT])
        YT = alloc(BF16, [16, NT])
        RB0 = cur[0]
        RB_SIZE = ARENA - RB0
        assert RB_SIZE >= 71808, RB_SIZE
        XT = at(RB0, F32, [16, NT])

        def pp(name, p1=128):
            o, w = PP[name]
            return T(sba, "sb", PPS.off + 4 * o, F32, [w], 0, p1)

        MEAN = at(YT.off, F32, [NT])
        RSTD = at(YT.off + 4096, F32, [NT])
        NMR = at(YT.off + 8192, F32, [NT])

        ring_ctr = [0]

        rec_plan = []
        issued = [0]

        def ring_src(key):
            kind = key[0]
            if kind == "win":
                return wcols(win_d[key[1]], 512 * key[2]), (16, 512)
            if kind == "wmod":
                return wcols(wmod_d[key[1]], 512 * key[2]), (16, 512)
            if kind == "wout":
                return wcols(wout_d[key[1]], 512 * key[2]), (16, 512)
            if kind == "wf":
                return wf_d[key[1]], (8, 1024)
            if kind == "gi":
                return gi_d[key[1]], (16, 512)
            raise KeyError(key)

        def ring_issue(k, key):
            src_ap, shape = ring_src(key)
            s = k % 2
            slot = at(RING[s].off, BF16, list(shape))
            step = shape[0] // 4
            for i in range(4):
                dst = slot[:, i * step:(i + 1) * step, :]
                DMA("pool", dst, src_ap[:, i * step:(i + 1) * step, :], "ring%d%s" % (s, "abcd"[i]),
                    reads=[], writes=[dst])

        def ring_load(key):
            k = ring_ctr[0]
            ring_ctr[0] += 1
            rec_plan.append(key)
            if plan is None:
                ring_issue(k, key)
            else:
                assert plan[k] == key, (k, plan[k], key)
                while issued[0] <= min(k + 1, len(plan) - 1):
                    ring_issue(issued[0], plan[issued[0]])
                    issued[0] += 1
            _, shape = ring_src(key)
            return at(RING[k % 2].off, BF16, list(shape))

        def wcols(w_ap, c0, n=512):
            return w_ap[:, c0:c0 + n].rearrange("(kc p) c -> p kc c", p=128)

        ldc = [0]

        def ld(dst, src, sem=None, q="sp"):
            if sem is None:
                sem = "c0"
            DMA(q, dst, src, sem, reads=[], writes=[dst])

        ld(IDF.all(), ident_d)
        ld(ONF.all(), ones_d)
        ld(ROPT.all(), ropeT_d)
        ld(LSELF.all(), lsel_d)
        ld(COS.all(), cos_d)
        ld(SIN.all(), sin_d)
        ld(MASKB.all(), maskb_d)
        ld(MINT.all(), mint_d)
        ld(SMASK.all(), smask_d)
        ld(CONDT.all(), cond_d)
        ld(FEAT.all(), featsT_d)
        CP("dve", IDB.all(), IDF.all())
        CP("dve", ONB.all(), ONF.all())
        CP("dve", LSEL.all(), LSELF.all())
        ACT(SCT.all(), CONDT.all(), AF.Silu)
        for i in range(4):
            dst = XT[:, 4 * i:4 * i + 4, :]
            DMA("sp", dst, xT_d[512 * i:512 * i + 512, :].rearrange("(kc p) t -> p kc t", p=128), "x",
                reads=[], writes=[dst])

        def ln_stats(X):
            for half in range(2):
                hs = slice(512 * half, 512 * half + 512)
                bS, bQ = nbank(), nbank()
                for kc in range(KC):
                    sq = STG[:, kc % 2, :]
                    ACT(sq, X[:, kc, hs], AF.Square)
                    MM(banks[bS].all(), ONF.all(), X[:, kc, hs], kc == 0, kc == KC - 1)
                    MM(banks[bQ].all(), ONF.all(), sq, kc == 0, kc == KC - 1)
                mean = MEAN[:, hs]
                S.add("act", lambda e, o=mean, i=banks[bS].all(): e.mul(out=o.ap, in_=i.ap, mul=1.0 / DM),
                      reads=[banks[bS].all()], writes=[mean])
                t = STG[:, 2, :]
                TT("dve", t, mean, mean, ALU.mult)
                STT("dve", t, banks[bQ].all(), 1.0 / DM, t, ALU.mult, ALU.subtract)
                ACT(t, t, AF.Sqrt, bias=LN_EPS, scale=1.0)
                RECIP("dve", RSTD[:, hs], t)
                STT("dve", NMR[:, hs], mean, -1.0, RSTD[:, hs], ALU.mult, ALU.mult)

        def ln_apply(X, out_fn, scale_fn, bias_fn):
            for half in range(2):
                for kc in range(KC):
                    hs = slice(512 * half, 512 * half + 512)
                    t = STG[:, 2 * (kc % 2), :]
                    TT("dve", t, X[:, kc, hs], RSTD[:, hs], ALU.mult)
                    t2 = STG[:, 2 * (kc % 2) + 1, :]
                    TT("pool", t2, t, NMR[:, hs], ALU.add)
                    ACT(out_fn(kc, hs), t2, AF.Identity, bias=bias_fn(kc), scale=scale_fn(kc))

        def proj_fm(slot, evac):
            for ft in range(4):
                for half in range(2):
                    b = nbank()
                    for kc in range(KC):
                        MM(banks[b].all(), slot[:, kc, 128 * ft:128 * ft + 128],
                           HT[:, kc, 512 * half:512 * half + 512], kc == 0, kc == KC - 1)
                    evac(ft, half, banks[b])

        def proj_tm(slot, evac):
            for tt in range(8):
                b = nbank()
                for kc in range(KC):
                    MM(banks[b].all(), HT[:, kc, 128 * tt:128 * tt + 128], slot[:, kc, :], kc == 0, kc == KC - 1)
                evac(tt, banks[b])

        def piece(l, p):
            return ring_load(("win", l, p))

        sgc = [0]
        if debug:
            MSET("pool", YT.all(), 0.0)

        def ck(k):
            S.phase = "L%s.ck%s" % (S.__dict__.get("layer", 0), k)
            if limit is not None and k == limit:
                raise _Stop()

        try:
          ck(1)
          for l in range(nlayers):
              lam_init = 0.8 - 0.6 * math.exp(-0.3 * l)
              S.layer = l
              S.phase = "L%d.mod" % l
              S.group_sems.update(["c0", "x", "pl%d" % l, "pq%d" % l, "spill%d" % l])
              ld(PPS.all(), pp_d[l], "pl%d" % l)
              DMA("pool", WST.all(), gw_d[l], "pq%d" % l, reads=[], writes=[WST.all()])
              DMA("pool", WPB.all(), pw_d[l], "pq%d" % l, reads=[], writes=[WPB.all()])
              DMA("pool", W3B.all(), w3_d[l], "pq%d" % l, reads=[], writes=[W3B.all()])
              BOUT, LNG, LNB, PSC = pp("bout"), pp("lng"), pp("lnb"), pp("psc")
              CONVW, CONVB = pp("convw"), pp("convb")
              GBS, HB, LQ = pp("gbs"), pp("hb"), pp("lq")

              BMT = at(YT.off + 28672, F32, [48])
              ld(BMT.all(), bmod_d[l], "pl%d" % l)
              AB2 = [at(HT.off, BF16, [2, 8, 512]), at(HT.off + 16384, BF16, [2, 8, 512])]
              FT = at(YT.off, F32, [8, 512])
              DECB = [at(YT.off + 16384, F32, [512]), at(YT.off + 18432, F32, [512])]
              KST = [at(YT.off + 20480, F32, [2, 512]), at(YT.off + 24576, F32, [2, 512])]
              H1 = at(STG.off + 4096, F32, [NT], 64)
              H2B = at(STG.off, BF16, [NT], 64)
              MR = at(STG.off + 2048, F32, [512], 1)

              def filter_chunks():
                  f2p = SMALL[0:64, 8:9]
                  c1 = SMALL[0:64, 9:10]
                  c2 = SMALL[0:64, 10:11]
                  TS("dve", f2p, pp("ffr", 64)[:, 0:1], 1.0 / TWO_PI, None, ALU.mult)
                  STT("dve", c1, pp("fb1", 64)[:, 0:1], 1.0, f2p, ALU.mult, ALU.mult)
                  STT("dve", c2, pp("fb2", 64)[:, 0:1], 1.0, f2p, ALU.mult, ALU.mult)
                  FW1 = pp("fw1", 33)
                  FW2 = pp("fw2", 64)
                  for (W, src_fn, cc, dst) in ((FW1, lambda hs: FEAT[0:33, hs], c1, H1), (FW2, lambda hs: H1[:, hs], c2, H2B)):
                      for half in range(2):
                          hs = slice(512 * half, 512 * half + 512)
                          b = nbank()
                          MM(banks[b][0:64, :], W.all(), src_fn(hs), True, True)
                          u = at(FT.off + 2048 * half, F32, [512], 64)
                          rn = at(FT.off + 4096 + 2048 * half, F32, [512], 64)
                          TS("dve", u.all(), banks[b][0:64, :], f2p, cc, ALU.mult, ALU.add)
                          TS("dve", rn.all(), u.all(), 12582912.0, 12582912.0, ALU.add, ALU.subtract)
                          TT("dve", u.all(), u.all(), rn.all(), ALU.subtract)
                          ACT(dst[:, hs], u.all(), AF.Sin, bias=0.0, scale=6.28318)
                      yield
                  decc = [0]

                  def dec_tile(tt):
                      k = decc[0] % 2
                      decc[0] += 1
                      DMA("sp", DECB[k].all(), dec_d[128 * tt:128 * tt + 128, :], "dec%d" % k, reads=[], writes=[DECB[k].all()])
                      return DECB[k]
                  for o in range(2):
                      bsum = nbank()
                      reserved.add(bsum)
                      def sums(tq):
                          for dr in range(2):
                              n_ = 2 * tq + dr
                              MM(banks[bsum].all(), ONF.all(), FT[:, 2 * (tq % 2) + dr, :], n_ == 0, n_ == 15)
                      for tt in range(8):
                          dct = dec_tile(tt)
                          for dr in range(2):
                              b = nbank()
                              cb = (2 * o + dr) * 512
                              MM(banks[b].all(), H2B[0:64, 128 * tt:128 * tt + 128], W3B[0:64, cb:cb + 512], True, True)
                              f = FT[:, 2 * (tt % 2) + dr, :]
                              STT("dve", f, banks[b].all(), SMASK[:, 8 * dr + tt:8 * dr + tt + 1], dct.all(), ALU.mult, ALU.mult)
                              ACT(f, f, AF.Abs)
                          if tt >= 1:
                              sums(tt - 1)
                          yield
                      sums(7)
                      rsum = FT[:, 6, :]
                      RECIP("dve", rsum, banks[bsum].all())
                      reserved.discard(bsum)
                      for tt in range(8):
                          dct = dec_tile(tt)
                          bF, bB = nbank(), nbank()
                          MM(banks[bF].all(), H2B[0:64, 128 * tt:128 * tt + 128], W3B[0:64, (2 * o) * 512:(2 * o) * 512 + 512], True, True)
                          MM(banks[bB].all(), H2B[0:64, 128 * tt:128 * tt + 128], W3B[0:64, (2 * o + 1) * 512:(2 * o + 1) * 512 + 512], True, True)
                          if tt % 2 == 0:
                              f, bw, sm_ = FT[:, 0, :], FT[:, 1, :], FT[:, 2, :]
                          else:
                              f, bw, sm_ = FT[:, 4, :], FT[:, 5, :], FT[:, 7, :]
                          TT("dve", f, banks[bF].all(), dct.all(), ALU.mult)
                          STT("dve", bw, banks[bB].all(), SMASK[:, 16 + tt:16 + tt + 1], dct.all(), ALU.mult, ALU.mult)
                          TT("dve", sm_, f, bw, ALU.add)
                          TT("dve", AB2[o][:, 0, tt, :], sm_, rsum, ALU.mult)
                          TT("dve", sm_, f, bw, ALU.subtract)
                          TT("dve", AB2[o][:, 1, tt, :], sm_, rsum, ALU.mult)
                          yield

              kstc = [0]

              WFS = at(YT.off, BF16, [8, 1024])

              def wfs_load(s_):
                  for i in range(2):
                      dst = WFS[:, 4 * i:4 * i + 4, :]
                      DMA("sp", dst, wf_d[s_][:, 4 * i:4 * i + 4, :], "wfs%s" % "ab"[i], reads=[], writes=[dst])

              def kspec_pair(i):
                  pl = i % 4
                  for o in range(2):
                      bKr, bKs = nbank(), nbank()
                      for kc in range(8):
                          MM(banks[bKr].all(), WFS[:, kc, 256 * pl:256 * pl + 128], AB2[o][:, 0, kc, :], kc == 0, kc == 7)
                          MM(banks[bKs].all(), WFS[:, kc, 256 * pl + 128:256 * pl + 256], AB2[o][:, 1, kc, :], kc == 0, kc == 7)
                      k = kstc[0] % 2
                      kstc[0] += 1
                      TT("dve", KST[k][:, 0, :], banks[bKr].all(), HB[:, 512 * o:512 * o + 512], ALU.add)
                      CP("act", KST[k][:, 1, :], banks[bKs].all())
                      DMA("sp", kp_d[o, i], KST[k].all(), "kst%d" % k, reads=[KST[k].all()],
                          writes=[dacc(kp_d, "kspec", 8 * o + i, 8 * o + i + 1)])

              gen = filter_chunks()
              bT = 7
              reserved.add(bT)

              def mod_slot(s_):
                  slot = ring_load(("wmod", l, s_))
                  b = nbank()
                  for kc in range(KC):
                      MM(banks[b][0:1, :], SCT[:, kc:kc + 1], slot[:, kc, :], kc == 0, kc == KC - 1)
                  CP("act", MR[0:1, :], banks[b][0:1, :])
                  for j in range(4):
                      MM(banks[bT][:, 4 * s_ + j:4 * s_ + j + 1], MR[0:1, 128 * j:128 * j + 128], ONF[0:1, 0:1], True, True)

              for s_ in range(12):
                  mod_slot(s_)
                  if s_ < 4:
                      for _ in range(13):
                          next(gen, None)
                  if s_ == 3:
                      for _ in gen:
                          pass
                      wfs_load(0)
                  if 4 <= s_ < 8:
                      kspec_pair(s_ - 4)
                      if s_ == 7:
                          wfs_load(1)
                  if s_ >= 8:
                      kspec_pair(s_ - 4)
              TT("dve", MODT.all(), banks[bT][:, 0:48], BMT.all(), ALU.add)
              reserved.discard(bT)
              SHIFT = MODT
              TS("dve", S1T.all(), MODT[:, 16:32], 1.0, None, ALU.add)
              TT("dve", GBT.all(), MODT[:, 32:48], BOUT.all(), ALU.mult)
              if debug and l == 0:
                  DMA("sp", dbg["mod"], MODT.all(), "out", reads=[MODT.all()], writes=[dacc(dbg["mod"], "dbg_mod")])

              ck(2)
              lam_t = SMALL[:, 0:1]
              negl = SMALL[:, 1:2]
              swv = SMALL[:, 2:3]
              tmpq = SMALL[:, 16:80]
              e1 = SMALL[:, 3:4]
              e2 = SMALL[:, 4:5]
              TT("dve", tmpq, LQ[:, 0:64], LQ[:, 64:128], ALU.mult)
              S.add("dve", lambda e, o=e1, i=tmpq: e.reduce_sum(out=o.ap, in_=i.ap, axis=AX.X), reads=[tmpq], writes=[e1])
              TT("dve", tmpq, LQ[:, 128:192], LQ[:, 192:256], ALU.mult)
              S.add("dve", lambda e, o=e2, i=tmpq: e.reduce_sum(out=o.ap, in_=i.ap, axis=AX.X), reads=[tmpq], writes=[e2])
              ACT(e1, e1, AF.Exp)
              ACT(e2, e2, AF.Exp)
              TT("dve", lam_t, e1, e2, ALU.subtract)
              TS("dve", negl, lam_t, lam_init, -1.0, ALU.add, ALU.mult)
              TS("dve", swv, pp("subln")[:, 0:1], 1.0 - lam_init, None, ALU.mult)

              ln_stats(XT)
              ln_apply(XT, lambda kc, hs: HT[:, kc, hs], lambda kc: S1T[:, kc:kc + 1], lambda kc: MODT[:, kc:kc + 1])
              if debug and l == 0:
                  DMA("sp", dbg["hT"], HT.all(), "out", reads=[HT.all()], writes=[dacc(dbg["hT"], "dbg_hT")])
              for i in range(4):
                  src = XT[:, 4 * i:4 * i + 4, :]
                  DMA("sp", xs_d[512 * i:512 * i + 512, :].rearrange("(kc p) t -> p kc t", p=128), src, "spill%d" % l,
                      reads=[src], writes=[dacc(xs_d, "xspill", i, i + 1)])

              ck(3)
              R0 = RB0

              def sgbuf():
                  sgc[0] += 1
                  return at(SGOFF + 2048 * (sgc[0] % 2), F32, [512])

              VLN = at(R0, BF16, [8, 512])
              MIX = at(R0 + 8192, F32, [4, NT])
              SGOFF = R0 + 8192 + 16384
              slot = piece(l, 1)

              def evac_av(tt, bank):
                  st6 = SMALL[:, 80:86]
                  mv = SMALL[:, 86:88]
                  rs = SMALL[:, 88:89]
                  S.add("dve", lambda e, o=st6, i=bank.all(): e.bn_stats(out=o.ap, in_=i.ap), reads=[bank.all()], writes=[st6])
                  S.add("dve", lambda e, o=mv, i=st6: e.bn_aggr(out=o.ap, in_=i.ap), reads=[st6], writes=[mv])
                  ACT(rs, SMALL[:, 87:88], AF.Sqrt, bias=LN_EPS, scale=1.0)
                  RECIP("dve", rs, rs)
                  TS("dve", VLN[:, tt, :], bank.all(), SMALL[:, 86:87], rs, ALU.subtract, ALU.mult)
              proj_tm(slot, evac_av)
              for g in range(4):
                  bb = [nbank(), nbank()]
                  for tt in range(8):
                      MM(banks[bb[tt // 4]][:, 128 * (tt % 4):128 * (tt % 4) + 128], VLN[:, tt, 128 * g:128 * g + 128],
                         WST[:, g, :], True, True)
                  for half in range(2):
                      for j in range(4):
                          TT("dve", MIX[:, g, 512 * half + 128 * j:512 * half + 128 * j + 128],
                             banks[bb[half]][:, 128 * j:128 * j + 128], GBS[:, 128 * g:128 * g + 128], ALU.add)
              slot = piece(l, 0)
              proj_fm(slot, lambda g, half, bank: TT("dve", MIX[:, g, 512 * half:512 * half + 512], bank.all(),
                                                     MIX[:, g, 512 * half:512 * half + 512], ALU.mult))
              slot = piece(l, 2)

              def evac_gate(ybase, SRC):
                  def f(g, half, bank):
                      sg = sgbuf()
                      ACT(sg.all(), bank.all(), AF.Silu)
                      TT("pool", YT[:, ybase + g, 512 * half:512 * half + 512], sg.all(),
                         SRC[:, g, 512 * half:512 * half + 512], ALU.mult)
                  return f
              proj_fm(slot, evac_gate(0, MIX))

              ck(4)
              PB = at(R0, F32, [1104])
              PBseg = at(R0 + 64, F32, [4, 272])
              SA = at(R0 + 4480, F32, [1104])
              SAseg = at(R0 + 4480 + 64, F32, [4, 272])
              SBf = at(R0 + 8960, F32, [1104])
              SBseg = at(R0 + 8960 + 64, F32, [4, 272])
              RCP = at(R0 + 13440, F32, [1104])
              DIFFS = [at(R0 + 17920 + 2048 * q, BF16, [4, 256]) for q in range(4)]
              DIFFLS = [at(R0 + 17920 + 2048 * q, BF16, [NT]) for q in range(4)]
              POOLED = at(R0 + 26624, F32, [4, NT])
              SGOFF = R0 + 26624 + 16384

              def pooled_mm(g):
                  for hh in range(2):
                      b = nbank()
                      MM(banks[b].all(), WPB[:, g, :], DIFFLS[g][:, 512 * hh:512 * hh + 512], True, True)
                      ACT(POOLED[:, g, 512 * hh:512 * hh + 512], banks[b].all(), AF.Identity, bias=0.0, scale=PSC[:, g:g + 1])
              MSET("pool", SA.all(), 0.0)
              MSET("pool", SBf.all(), 0.0)
              slot = piece(l, 3)

              def evac_bx(g, half, bank):
                  if half == 0:
                      MSET("pool", PB.all(), 0.0)
                  for sl in range(2):
                      s = 2 * half + sl
                      CP("act", PB[:, 16 + 272 * s:16 + 272 * s + 256], bank[:, 256 * sl:256 * sl + 256])
                  if half == 1:
                      for s in range(1, 4):
                          TS("pool", PB[:, 8 + 272 * s:16 + 272 * s], PB[:, 16 + 272 * (s - 1) + 248:16 + 272 * (s - 1) + 256],
                             MINT[:, 0:1], None, ALU.mult)
                      for s in range(0, 3):
                          TS("pool", PB[:, 8 + 272 * s + 264:8 + 272 * s + 272], PB[:, 16 + 272 * (s + 1):16 + 272 * (s + 1) + 8],
                             MINT[:, 0:1], None, ALU.mult)
                      lo, hi = 8, 1096
                      TT("pool", SA[:, lo:hi], PB[:, lo - 1:hi - 1], PB[:, lo:hi], ALU.add)
                      src, dst, sseg, dseg = SA, SBf, SAseg, SBseg
                      for k in range(g):
                          sh = 1 << k
                          TT("dve" if k % 2 == 0 else "pool", dst[:, lo:hi], src[:, lo - sh:hi - sh], src[:, lo + sh:hi + sh], ALU.add)
                          src, dst, sseg, dseg = dst, src, dseg, sseg
                      DMA("sp", RCP.all(), rcpad_d[g], "misc", reads=[], writes=[RCP.all()])
                      TT("pool", dst[:, lo:hi], src[:, lo:hi], RCP[:, lo:hi], ALU.mult)
                      TT("dve", DIFFS[g].all(), dseg[:, :, 0:256], PBseg[:, :, 0:256], ALU.subtract)
              proj_fm(slot, evac_bx)
              for g_ in range(4):
                  pooled_mm(g_)
              slot = piece(l, 4)
              proj_fm(slot, evac_gate(4, POOLED))

              ck(5)
              QT = at(R0, BF16, [4, NT])
              KT = at(R0 + 8192, BF16, [4, 1536])
              VB = at(R0 + 20480, BF16, [12, 512])
              O32 = at(R0 + 32768, F32, [4, NT])
              KRAW = O32
              TMP = at(R0 + 49152, F32, [8, 512])
              PTB = at(R0 + 65536, BF16, [4, 512])
              SGOFF = R0 + 49152 + 8192
              CST = at(TMP.off + 8192, F32, [4, 512])
              DMA("sp", CST.all(), ckT_d[l].rearrange("(h p) k -> p h k", p=128), "ck", reads=[], writes=[CST.all()])
              for h in range(4):
                  CP("dve" if h % 2 else "act", KT[:, h, 0:512], CST[:, h, :])
              DMA("sp", CST.all(), cv_d[l].rearrange("(t p) c -> p t c", p=128), "ck", reads=[], writes=[CST.all()])
              for h in range(4):
                  CP("dve" if h % 2 else "act", VB[:, h, :], CST[:, h, :])

              pend = []

              def rope_finish():
                  if pend:
                      q32, hs, dst = pend.pop(0)
                      b = nbank()
                      MM(banks[b].all(), ROPT.all(), q32, True, True)
                      t1 = TMP[:, 2, :]
                      t2 = TMP[:, 3, :]
                      TT("dve", t1, q32, COS[:, hs], ALU.mult)
                      TT("dve", t2, banks[b].all(), SIN[:, hs], ALU.mult)
                      TT("pool", dst, t1, t2, ALU.add)

              def rope_evac(dst_fn, raw_fn):
                  def f(h, half, bank):
                      hs = slice(512 * half, 512 * half + 512)
                      q32 = raw_fn(h, half, hs)
                      CP("act", q32, bank.all())
                      rope_finish()
                      pend.append((q32, hs, dst_fn(h, hs)))
                  return f
              slot = piece(l, 5)
              proj_fm(slot, rope_evac(lambda h, hs: QT[:, h, hs], lambda h, half, hs: TMP[:, half, :]))
              slot = piece(l, 6)
              proj_fm(slot, rope_evac(lambda h, hs: KT[:, h, 512 + hs.start:512 + hs.stop], lambda h, half, hs: KRAW[:, h, hs]))
              rope_finish()
              DMA("sp", kT_o[l].rearrange("(h p) t -> p h t", p=128), KRAW.all(), "kout", reads=[KRAW.all()],
                  writes=[dacc(kT_o, "kT_out")])
              ck(51)
              slot = piece(l, 7)

              def evac_v(tt, bank):
                  import os
                  st = STG[:, tt % 4, :]
                  if os.environ.get("DBG_V", "") != "noact":
                      CP("act", st, bank.all())
                  if os.environ.get("DBG_V", "") not in ("nodma", "noact"):
                      DMA("sp", v_o[l, 128 * tt:128 * tt + 128, :], st, "vout%d" % (tt % 4), reads=[st], writes=[dacc(v_o, "v_out", tt, tt + 1)])
                  CP("pool", VB[:, 4 + tt, :], st)
              proj_tm(slot, evac_v)
              ck(52)
              MX = SMALL[:, 96:128]
              for h in range(4):
                  col = 0
                  blocks = [(QT, h, 512 * i) for i in range(2)] + [(KT, h, 512 * i) for i in range(3)]
                  for (src, hh, c0) in blocks:
                      sq = PTB[:, col % 4, :]
                      TT("pool", sq, src[:, hh, c0:c0 + 512], src[:, hh, c0:c0 + 512], ALU.mult)
                      for m in range(2):
                          b = nbank()
                          MM(banks[b].all(), LSEL[:, m, :], sq, True, True)
                          o = SMALL[:, 96 + m * 8 + col:96 + m * 8 + col + 1]
                          S.add("dve", lambda e, o=o, i=banks[b].all(): e.reduce_max(out=o.ap, in_=i.ap, axis=AX.X),
                                reads=[banks[b].all()], writes=[o])
                      col += 1
                  for m in range(2):
                      base = 96 + m * 8
                      qm = SMALL[:, base + 5:base + 6]
                      km = SMALL[:, base + 6:base + 7]
                      TT("dve", qm, SMALL[:, base:base + 1], SMALL[:, base + 1:base + 2], ALU.max)
                      TT("dve", km, SMALL[:, base + 2:base + 3], SMALL[:, base + 3:base + 4], ALU.max)
                      TT("dve", km, km, SMALL[:, base + 4:base + 5], ALU.max)
                      TT("dve", qm, qm, km, ALU.mult)
                      ACT(qm, qm, AF.Sqrt, bias=0.0, scale=(1.02 / 8.0) ** 2)
                      TS("dve", qm, qm, -1.0, None, ALU.mult)
                      TS("dve", BT[:, 2 * h + m, :], MASKB.all(), qm, None, ALU.add)
              ck(53)
              groups = [(h, half) for h in range(4) for half in range(2)]
              steps = [(g, m, kt) for g in range(8) for m in range(2) for kt in range(12)]
              OB = [3, 4]
              SBK = [5, 6]
              SCB = [0, 1, 2, 7]

              QZ = at(TMP.off + 8192, BF16, [4, 512])
              MSET("pool", QZ.all(), 0.0)
              SACC = [TMP[:, 6, :], TMP[:, 7, :]]
              SACCP = [STG[:, 0, :], STG[:, 1, :]]

              def emit_qk(i):
                  g, m, kt = steps[i]
                  h, half = groups[g]
                  hs = slice(512 * half, 512 * half + 512)
                  qz = 2 * (g % 2) + m
                  if kt == 0:
                      CP("dve", QZ[64 * m:64 * m + 64, qz, :], QT[64 * m:64 * m + 64, h, hs])
                  MM(banks[SCB[i % 4]].all(), KT[:, h, 128 * kt:128 * kt + 128], QZ[:, qz, :], True, True)

              def emit_exp_pv(i):
                  g, m, kt = steps[i]
                  h, half = groups[g]
                  pk = i % 4
                  for sl in range(2):
                      sg_ = 2 * half + sl
                      ACT(PTB[:, pk, 256 * sl:256 * sl + 256], banks[SCB[i % 4]][:, 256 * sl:256 * sl + 256], AF.Exp,
                          bias=BT[:, 2 * h + m, kt * 4 + sg_:kt * 4 + sg_ + 1], scale=0.125)
                  MM(banks[OB[m]].all(), VB[:, kt, 128 * h:128 * h + 128], PTB[:, pk, :], kt == 0, kt == 11)
                  pt_ = PTB[:, pk, :]
                  if kt == 0:
                      CP("dve", SACC[m], pt_)
                  elif kt == 1:
                      CP("pool", SACCP[m], pt_)
                  elif kt % 2 == 0:
                      TT("dve", SACC[m], SACC[m], pt_, ALU.add)
                  else:
                      acc_ = SACCP[m]
                      S.add("pool", lambda e, o=acc_, a=acc_, b_=pt_: e.tensor_tensor(out=o.ap, in0=a.ap, in1=b_.ap, op=ALU.add),
                            reads=[acc_, pt_], writes=[acc_])
                  if kt == 11:
                      MM(banks[SBK[m]].all(), ONF.all(), SACC[m], True, False)
                      MM(banks[SBK[m]].all(), ONF.all(), SACCP[m], False, True)

              def combine1(g):
                  h, half = groups[g]
                  hs = slice(512 * half, 512 * half + 512)
                  c = [TMP[:, k, :] for k in range(4)]
                  CP("act", c[0], banks[OB[0]].all())
                  CP("act", c[1], banks[OB[1]].all())
                  ACT(c[2], banks[SBK[0]].all(), AF.Ln)
                  ACT(c[3], banks[SBK[1]].all(), AF.Ln)
                  ACT(c[2], c[2], AF.Exp, bias=0.0, scale=-1.0)
                  ACT(c[3], c[3], AF.Exp, bias=0.0, scale=-1.0)
                  TT("dve", c[0], c[0], c[2], ALU.mult)
                  TT("pool", c[1], c[1], c[3], ALU.mult)
                  STT("dve", O32[:, h, hs], c[1], negl, c[0], ALU.mult, ALU.add)
                  TT("pool", c[2], O32[:, h, hs], O32[:, h, hs], ALU.mult)

              def combine2(g):
                  h, half = groups[g]
                  hs = slice(512 * half, 512 * half + 512)
                  r = TMP[:, 3, :]
                  MM(banks[5].all(), ONF.all(), TMP[:, 2, :], True, True)
                  ACT(r, banks[5].all(), AF.Ln, bias=1e-5, scale=1.0 / 128.0)
                  ACT(r, r, AF.Exp, bias=0.0, scale=-0.5)
                  STT("dve", O32[:, h, hs], O32[:, h, hs], swv, r, ALU.mult, ALU.mult)

              NS = len(steps)
              for i in range(NS + 3):
                  if i < NS:
                      emit_qk(i)
                  j = i - 3
                  if j >= 0:
                      emit_exp_pv(j)
                      g, m, kt = steps[j]
                      if m == 1 and kt == 11:
                          combine1(g)
                      if m == 0 and kt == 8 and g >= 1:
                          combine2(g - 1)
              combine2(7)
              slot = piece(l, 8)
              proj_fm(slot, evac_gate(8, O32))

              ck(6)
              G = at(R0, F32, [4, NT])
              Gseg = at(R0, F32, [4, 4, 256])
              ZT = at(R0 + 16384, BF16, [8, 512])
              Y = at(R0 + 24576, BF16, [16, 512])
              KP = [at(R0 + 40960, F32, [2, 512]), at(R0 + 45056, F32, [2, 512])]
              MTS = [at(R0 + 49152, F32, [4, 512]), at(R0 + 57344, F32, [4, 512])]
              PC = at(R0 + 49152, F32, [1040])
              PCp = at(R0 + 49152 + 12, F32, [4, 258])
              ACC = at(R0 + 49152 + 4160, F32, [1040])
              ACCs = at(R0 + 49152 + 4160 + 8, F32, [4, 258])
              SGOFF = R0 + 57344 + 4096

              def conv3(p, cbase):
                  S.phase = "L%d.D.conv3" % l
                  slot = piece(l, p)
                  MSET("pool", PC.all(), 0.0)

                  def ev(ft, half, bank):
                      for sl in range(2):
                          s = 2 * half + sl
                          CP("act", PC[:, 2 + 258 * s:2 + 258 * s + 256], bank[:, 256 * sl:256 * sl + 256])
                      if half == 1:
                          for s in range(1, 4):
                              TS("pool", PC[:, 1 + 258 * s:2 + 258 * s], PC[:, 2 + 258 * (s - 1) + 255:2 + 258 * (s - 1) + 256],
                                 MINT[:, 0:1], None, ALU.mult)
                          for s in range(0, 3):
                              TS("pool", PC[:, 258 * s + 258:258 * s + 259], PC[:, 2 + 258 * (s + 1):3 + 258 * (s + 1)],
                                 MINT[:, 0:1], None, ALU.mult)
                          ci = cbase + ft
                          ACT(ACC[:, 1:1033], PC[:, 1:1033], AF.Identity, bias=CONVB[:, ci:ci + 1], scale=CONVW[:, 12 + ci:13 + ci])
                          STT("dve", ACC[:, 1:1033], PC[:, 0:1032], CONVW[:, ci:ci + 1], ACC[:, 1:1033], ALU.mult, ALU.add)
                          STT("pool", Gseg[:, ft, :, :], PCp[:, :, 0:256], CONVW[:, 24 + ci:25 + ci], ACCs[:, :, 0:256], ALU.mult, ALU.add)
                  proj_fm(slot, ev)

              def to_time_major():
                  S.phase = "L%d.D.tr" % l
                  for ft in range(4):
                      for hh in range(2):
                          b = nbank()
                          for j in range(4):
                              tt = 4 * hh + j
                              TR(banks[b][:, 128 * j:128 * j + 128], G[:, ft, 128 * tt:128 * tt + 128], IDF.all())
                          src = T(psa, "ps", 2048 * b, F32, [4, 128])
                          CP("act" if hh == 0 else "dve", ZT[:, 4 * hh:4 * hh + 4, 128 * ft:128 * ft + 128], src.all())

              def long_conv(o):
                  S.phase = "L%d.D.fwd" % l
                  for s in range(2):
                      wslot = ring_load(("wf", s))
                      for pl in range(4):
                          i = 4 * s + pl
                          k = i % 2
                          DMA("sp", KP[k].all(), kp_d[o, i], "kp%d" % k,
                              reads=[dacc(kp_d, "kspec", 8 * o + i, 8 * o + i + 1)], writes=[KP[k].all()])
                          bZr, bZs = nbank(), nbank()
                          for kc in range(8):
                              MM(banks[bZr].all(), wslot[:, kc, 256 * pl:256 * pl + 128], ZT[:, kc, :], kc == 0, kc == 7)
                              MM(banks[bZs].all(), wslot[:, kc, 256 * pl + 128:256 * pl + 256], ZT[:, kc, :], kc == 0, kc == 7)
                          kre, ks = KP[k][:, 0, :], KP[k][:, 1, :]
                          zre, zs, t1, t2 = (MTS[k][:, q, :] for q in range(4))
                          CP("act", zre, banks[bZr].all())
                          CP("act", zs, banks[bZs].all())
                          TT("dve", t1, zre, kre, ALU.mult)
                          TT("pool", t2, zs, ks, ALU.mult)
                          TT("dve", Y[:, i, :], t1, t2, ALU.subtract)
                          TT("dve", zre, zre, ks, ALU.mult)
                          TT("pool", zs, zs, kre, ALU.mult)
                          TT("dve", Y[:, 8 + i, :], zre, zs, ALU.add)
                  S.phase = "L%d.D.inv" % l
                  for half in range(2):
                      gslot = ring_load(("gi", half))
                      for ft in range(4):
                          b = nbank()
                          for kc in range(16):
                              MM(banks[b].all(), Y[:, kc, 128 * ft:128 * ft + 128], gslot[:, kc, :], kc == 0, kc == 15)
                          TT("dve", G[:, ft, 512 * half:512 * half + 512], banks[b].all(), G[:, ft, 512 * half:512 * half + 512], ALU.mult)

              conv3(11, 8)
              to_time_major()
              conv3(9, 0)
              long_conv(0)
              to_time_major()
              conv3(10, 4)
              long_conv(1)
              slot = piece(l, 12)
              proj_fm(slot, evac_gate(12, G))

              ck(7)
              XR = [at(STG.off, F32, [NT]), at(STG.off + 4096, F32, [NT])]
              ET = [at(HT.off, F32, [512]), at(HT.off + 2048, F32, [512])]
              ec = 0
              for s in range(4):
                  slot = ring_load(("wout", l, s))
                  for j in range(4):
                      jt = 4 * s + j
                      xr = XR[jt % 2]
                      DMA("sp", xr.all(), xs_d[128 * jt:128 * jt + 128, :], "xr%d" % (jt % 2),
                          reads=[dacc(xs_d, "xspill", jt // 4, jt // 4 + 1)], writes=[xr.all()])
                      for half in range(2):
                          hs = slice(512 * half, 512 * half + 512)
                          b = nbank()
                          for kc in range(KC):
                              MM(banks[b].all(), slot[:, kc, 128 * j:128 * j + 128], YT[:, kc, hs], kc == 0, kc == KC - 1)
                          et = ET[ec % 2]
                          ec += 1
                          ACT(et.all(), banks[b].all(), AF.Identity, bias=GBT[:, jt:jt + 1], scale=MODT[:, 32 + jt:33 + jt])
                          STT("dve", XT[:, jt, hs], xr[:, hs], ALPHA, et.all(), ALU.mult, ALU.add)
              S.phase = "L%d.E.ln" % l
              ln_stats(XT)
              ln_apply(XT, lambda kc, hs: XT[:, kc, hs], lambda kc: LNG[:, kc:kc + 1], lambda kc: LNB[:, kc:kc + 1])


        except _Stop:
            pass
        if debug:
            DMA("sp", dbg["yT"], YT.all(), "out", reads=[YT.all()], writes=[dacc(dbg["yT"], "dbg_yT")])
        for i in range(4):
            src = XT[:, 4 * i:4 * i + 4, :]
            DMA("sp", yT_o[512 * i:512 * i + 512, :].rearrange("(kc p) t -> p kc t", p=128), src, "out",
                reads=[src], writes=[dacc(yT_o, "yT_out")])

        for n in sorted(S.dma_counts):
            dsems[n] = es.enter_context(nc.semaphore("d_" + n))
        with nc.Block() as block:
            @block.tensor
            def _(eng):
                S.emit_one("pe", eng, sems, dsems)

            @block.scalar
            def _(eng):
                S.emit_one("act", eng, sems, dsems)

            @block.vector
            def _(eng):
                S.emit_one("dve", eng, sems, dsems)

            @block.gpsimd
            def _(eng):
                S.emit_one("pool", eng, sems, dsems)

            @block.sync
            def _(eng):
                S.emit_one("sp", eng, sems, dsems, final=True)
    nc._sched = S
    nc._ring_plan = rec_plan
    return nc


_CACHE = {}


def make_in_maps(inp):
    f32 = lambda a: np.ascontiguousarray(np.asarray(a, dtype=np.float32))
    inp = {k: np.asarray(v) for k, v in inp.items()}
    fixed = _fixed_consts()
    tabs = {"s": _tables("s"), "p": _tables("p")}
    shared = {
        "w_mod": f32(inp["w_mod"]),
        "b_modT": f32(np.transpose(inp["b_mod"].reshape(DEPTH, 48, 128), (0, 2, 1))),
        "w_in": f32(inp["w_in"]),
        "w_out": f32(inp["w_out"]),
        "gmlp_wT": f32(np.transpose(inp["gmlp_w"], (0, 3, 1, 2))),
        "pool_w": f32(np.transpose(inp["pool_w"], (0, 2, 1, 3))),
        "filt_w3": f32(inp["filt_w3"]),
        "pp": f32(np.stack([_pack_params(inp, l) for l in range(DEPTH)])),
    }
    shared.update(fixed)
    maps = []
    for core in range(8):
        m = dict(shared)
        if core < 4:
            b = core
            kind = "s"
            x = inp["x_sample"][b]
            cond = inp["c"][b]
            ck = inp["cache_k"][b].reshape(DEPTH, 512, 512)
            cv = inp["cache_v"][b].reshape(DEPTH, 512, 512)
        else:
            kind = "p"
            x = inp["x_prompt"][4 * (core - 4):4 * (core - 4) + 4].reshape(NT, DM)
            cond = inp["c_ctx"]
            ck = np.zeros((DEPTH, 512, 512), np.float32)
            cv = np.zeros((DEPTH, 512, 512), np.float32)
        m["xT"] = f32(x.T)
        m["cond"] = f32(cond.reshape(16, 128).T)
        m["ckT"] = f32(np.transpose(ck, (0, 2, 1)))
        m["cv"] = f32(cv)
        for k, v in tabs[kind].items():
            m[k] = v
        maps.append(m)
    return maps


def kernel(**inputs):
    if "nc" not in _CACHE:
        _CACHE["nc"] = build_program()
    nc = _CACHE["nc"]
    maps = make_in_maps(inputs)
    res = run_bass_kernel_spmd(nc, maps, core_ids=list(range(8)))
    r = res.results
    y_sample = np.stack([np.ascontiguousarray(r[c]["yT_out"].T) for c in range(4)]).astype(np.float32)
    y_prompt = np.concatenate([np.ascontiguousarray(r[c]["yT_out"].T).reshape(4, 256, DM) for c in range(4, 8)]).astype(np.float32)
    nk = np.zeros((16, DEPTH, 256, 4, 2, 64), np.float32)
    nv = np.zeros((16, DEPTH, 256, 4, 128), np.float32)
    for c in range(4, 8):
        kt = r[c]["kT_out"]
        vv = r[c]["v_out"]
        for l in range(DEPTH):
            kk = np.ascontiguousarray(kt[l].T).reshape(4, 256, 4, 2, 64)
            nk[4 * (c - 4):4 * (c - 4) + 4, l] = kk
            nv[4 * (c - 4):4 * (c - 4) + 4, l] = vv[l].reshape(4, 256, 4, 128)
    return (y_prompt, y_sample, nk, nv)
```

```python
import contextlib
import math
import numpy as np
import ml_dtypes
import concourse.bass as bass
import concourse.mybir as mybir
from concourse.bass_utils import run_bass_kernel_spmd

F32 = mybir.dt.float32
BF16 = mybir.dt.bfloat16
ESZ = {F32: 4, BF16: 2}
AF = mybir.ActivationFunctionType
ALU = mybir.AluOpType
AX = mybir.AxisListType

DEPTH = 2
NT = 1024
DM = 2048
KC = 16
NPIECE = 13
LN_EPS = 1e-6
ALPHA = (2.0 * DEPTH) ** 0.25
TWO_PI = 2.0 * math.pi


class Acc:
    __slots__ = ("ap", "space", "lo", "hi", "p0", "p1")

    def __init__(self, ap, space, lo, hi, p0, p1):
        self.ap, self.space, self.lo, self.hi, self.p0, self.p1 = ap, space, lo, hi, p0, p1


class T:
    def __init__(self, arena_ap, space, off, dtype, shape, p0=0, pn=128):
        self.space, self.off, self.dtype, self.shape = space, off, dtype, list(shape)
        self.p0, self.pn = p0, pn
        es = ESZ[dtype]
        n = int(np.prod(shape))
        nbytes = n * es
        assert off % 4 == 0 and nbytes % 4 == 0, (off, nbytes)
        ap = arena_ap[p0:p0 + pn, off // 4:(off + nbytes) // 4]
        if dtype != F32:
            ap = ap.bitcast(dtype)
        if len(shape) == 2:
            ap = ap.rearrange("p (a b) -> p a b", b=shape[1])
        elif len(shape) == 3:
            ap = ap.rearrange("p (a b c) -> p a b c", b=shape[1], c=shape[2])
        self.ap = ap
        self.es = es
        st = []
        acc = 1
        for d in reversed(self.shape):
            st.append(acc)
            acc *= d
        self.strides = list(reversed(st))
        self.nbytes = nbytes

    def __getitem__(self, idx):
        if not isinstance(idx, tuple):
            idx = (idx,)
        pidx = idx[0]
        fidx = list(idx[1:]) + [slice(None)] * (len(self.shape) - len(idx) + 1)
        if isinstance(pidx, slice):
            ps = 0 if pidx.start is None else pidx.start
            pe = self.pn if pidx.stop is None else pidx.stop
        else:
            ps, pe = pidx, pidx + 1
        pidx = slice(ps, pe)
        lo = 0
        hi = 0
        for k, ix in enumerate(fidx):
            d = self.shape[k]
            if isinstance(ix, slice):
                s = 0 if ix.start is None else ix.start
                e = d if ix.stop is None else ix.stop
                last = e - 1
            else:
                s, last = ix, ix
            assert 0 <= s <= last < d, (self.shape, idx)
            lo += s * self.strides[k]
            hi += last * self.strides[k]
        ap = self.ap[(pidx,) + tuple(fidx)]
        blo, bhi = self.off + lo * self.es, self.off + (hi + 1) * self.es
        if self.space == "ps":
            blo = (blo // 2048) * 2048
            bhi = ((bhi + 2047) // 2048) * 2048
            return Acc(ap, self.space, blo, bhi, 0, 128)
        return Acc(ap, self.space, blo, bhi, self.p0 + ps, self.p0 + pe)

    def all(self):
        return self[tuple([slice(None)] * (len(self.shape) + 1))]


def dacc(ap, name, lo=0, hi=1 << 40):
    return Acc(ap, "dram:" + name, lo, hi, 0, 1)


class Op:
    __slots__ = ("eng", "fn", "deps", "signal", "sigval", "dma_sem", "dma_val", "tag", "waits")


class Sched:
    ENGS = ("pe", "act", "dve", "pool", "sp")

    def __init__(self):
        self.ops = {e: [] for e in self.ENGS}
        self.recs = {}
        self.dma_counts = {}
        self.group_sems = set()
        self.all_ops = []
        self._fin = False

    @staticmethod
    def _ov(r, a):
        return r[0] < a.hi and a.lo < r[1] and r[2] < a.p1 and a.p0 < r[3]

    def add(self, eng, fn, reads=(), writes=(), dma_sem=None):
        op = Op()
        op.eng, op.fn, op.signal, op.sigval = eng, fn, False, 0
        op.dma_sem, op.dma_val = dma_sem, 0
        op.tag = getattr(self, "phase", "")
        deps = []
        for a in reads:
            for r in self.recs.setdefault(a.space, []):
                if self._ov(r, a):
                    if r[4] is not None:
                        deps.append(r[4])
                    if a.space == "ps":
                        deps.extend(x for x in r[5] if x.eng != eng)
                    r[5].append(op)
        for a in writes:
            recs = self.recs.setdefault(a.space, [])
            keep = []
            for r in recs:
                if self._ov(r, a):
                    if r[4] is not None:
                        deps.append(r[4])
                    deps.extend(r[5])
                    covered = (a.lo <= r[0] and r[1] <= a.hi and a.p0 <= r[2] and r[3] <= a.p1)
                    if not covered:
                        keep.append(r)
                else:
                    keep.append(r)
            keep.append([a.lo, a.hi, a.p0, a.p1, op, []])
            self.recs[a.space] = keep
        seen = set()
        od = []
        for d in deps:
            if id(d) in seen or d is op:
                continue
            seen.add(id(d))
            if d.dma_sem is None and d.eng == eng and eng in ("pe", "sp"):
                continue
            od.append(d)
            if d.dma_sem is None:
                d.signal = True
        op.deps = od
        if dma_sem is not None:
            c = self.dma_counts.get(dma_sem, 0) + 1
            self.dma_counts[dma_sem] = c
            op.dma_val = 16 * c
        self.ops[eng].append(op)
        self.all_ops.append(op)
        return op

    def _plan_waits(self):
        rcs = {e: {} for e in self.ENGS}
        clock_at = {}
        for op in self.all_ops:
            rc = rcs[op.eng]
            waits = {}
            for d in op.deps:
                if d.dma_sem is not None:
                    k, v = ("dma", d.dma_sem), d.dma_val
                    if d.dma_sem in self.group_sems:
                        v = 16 * self.dma_counts[d.dma_sem]
                else:
                    k, v = ("eng", d.eng), d.sigval
                if rc.get(k, 0) >= v:
                    continue
                if waits.get(k, 0) < v:
                    waits[k] = v
                rc[k] = v
                for kk, vv in clock_at[id(d)].items():
                    if rc.get(kk, 0) < vv:
                        rc[kk] = vv
            op.waits = list(waits.items())
            snap = dict(rc)
            if op.dma_sem is not None:
                snap[("dma", op.dma_sem)] = max(snap.get(("dma", op.dma_sem), 0), op.dma_val)
            elif op.signal:
                snap[("eng", op.eng)] = max(snap.get(("eng", op.eng), 0), op.sigval)
            clock_at[id(op)] = snap

    def emit_one(self, e, eng, sems, dma_sems, final=False):
        if not self._fin:
            for ee in self.ENGS:
                c = 0
                for op in self.ops[ee]:
                    if op.signal:
                        c += 1
                        op.sigval = c
            self._fin = True
        if not getattr(self, "_planned", False):
            self._plan_waits()
            self._planned = True
        for op in self.ops[e]:
            for k, v in op.waits:
                s = dma_sems[k[1]] if k[0] == "dma" else sems[k[1]]
                eng.wait_ge(s, v)
            ins = op.fn(eng)
            if op.dma_sem is not None:
                ins.then_inc(dma_sems[op.dma_sem], 16)
            elif op.signal:
                ins.then_inc(sems[e], 1)
        if final:
            for name, c in self.dma_counts.items():
                eng.wait_ge(dma_sems[name], 16 * c)


def _bf(a):
    return np.ascontiguousarray(a.astype(ml_dtypes.bfloat16))


def _tables(kind):
    L = 1024 if kind == "s" else 256
    nseg_seq = NT // L
    t = np.arange(NT)
    tl = t % L
    seg = t // L
    tb = {}
    p = np.arange(128)
    d = p % 64
    j = d % 16
    inv = 10000.0 ** (-(2.0 * j) / 32.0)
    if kind == "s":
        pos = np.where((d // 32 == 0)[:, None], (t // 64)[None, :], (t % 64)[None, :]).astype(np.float64)
        ang = pos * inv[:, None]
        cosT, sinT = np.cos(ang), np.sin(ang)
    else:
        cosT, sinT = np.ones((128, NT)), np.zeros((128, NT))
    tb["cos"] = cosT.astype(np.float32)
    tb["sin"] = sinT.astype(np.float32)
    mk = np.zeros((128, 48), np.float32)
    if kind == "p":
        for kt in range(12):
            for sg in range(4):
                ok = kt >= 4 and (kt - 4) // 2 == sg
                mk[:, kt * 4 + sg] = 0.0 if ok else -30000.0
    tb["maskb"] = mk
    rc = np.zeros((4, 1104), np.float32)
    for g, w in enumerate((2, 4, 8, 16)):
        lo = np.clip(tl - w // 2, 0, L)
        hi = np.clip(tl + w // 2, 0, L)
        cnt = (hi - lo).astype(np.float32)
        for s in range(4):
            rc[g, 16 + 272 * s:16 + 272 * s + 256] = 1.0 / cnt[256 * s:256 * s + 256]
    tb["rcpad"] = np.ascontiguousarray(np.broadcast_to(rc[:, None, :], (4, 128, 1104)))
    tn = np.linspace(0.0, 1.0, L, dtype=np.float32)
    bands = np.linspace(1e-4, 15.0, 16, dtype=np.float32)
    ang = (np.float32(2.0 * math.pi) * np.arange(L, dtype=np.float32) / np.float32(L))[:, None] * bands[None, :]
    feats = np.concatenate([tn[:, None], np.cos(ang), np.sin(ang)], axis=-1).astype(np.float32)
    feats = np.tile(feats, (nseg_seq, 1))
    ft = np.zeros((64, NT), np.float32)
    ft[:33] = feats.T
    tb["featsT"] = ft
    deltas = np.abs(np.linspace(math.log(1e-2) / 0.3, math.log(1e-2) / 1.5, 512, dtype=np.float32))
    dec = np.exp(-tn[:, None] * deltas[None, :]).astype(np.float32)
    tb["dec"] = np.ascontiguousarray(np.tile(dec, (nseg_seq, 1)))
    bm = (tl != 0).astype(np.float32)
    ms = (t < L).astype(np.float32)
    sm = np.zeros((128, 24), np.float32)
    sm[:, 0:8] = ms.reshape(8, 128).T
    sm[:, 8:16] = (ms * bm).reshape(8, 128).T
    sm[:, 16:24] = bm.reshape(8, 128).T
    tb["smask"] = sm
    f = np.arange(NT)
    fl = f % L
    om = math.pi * (fl + 0.5) / L
    same = (seg[:, None] == (f // L)[None, :])
    ph = tl[:, None].astype(np.float64) * om[None, :]
    C = np.where(same, np.cos(ph), 0.0)
    Sn = np.where(same, np.sin(ph), 0.0)
    wf = np.zeros((2, 128, 8, 1024), np.float32)
    for s in range(2):
        for pl in range(4):
            fr = (4 * s + pl) * 128
            wf[s, :, :, pl * 256:pl * 256 + 128] = C[:, fr:fr + 128].reshape(8, 128, 128).transpose(1, 0, 2)
            wf[s, :, :, pl * 256 + 128:pl * 256 + 256] = Sn[:, fr:fr + 128].reshape(8, 128, 128).transpose(1, 0, 2)
    tb["wf"] = _bf(wf)
    G = np.concatenate([C.T, Sn.T], axis=0) / L
    gi = np.zeros((2, 128, 16, 512), np.float32)
    for h in range(2):
        gi[h] = G[:, 512 * h:512 * h + 512].reshape(16, 128, 512).transpose(1, 0, 2)
    tb["gi"] = _bf(gi)
    tb["mint"] = np.full((128, 1), 1.0 if kind == "s" else 0.0, np.float32)
    return tb


def _fixed_consts():
    c = {}
    c["ident"] = np.eye(128, dtype=np.float32)
    c["ones"] = np.ones((128, 128), np.float32)
    p = np.arange(128)
    d = p % 64
    R = np.zeros((128, 128), np.float32)
    for pp in range(128):
        if (d[pp] % 32) < 16:
            R[pp, pp + 16] = -1.0
        else:
            R[pp, pp - 16] = 1.0
    c["ropeT"] = np.ascontiguousarray(R.T)
    l0 = np.zeros((128, 128), np.float32)
    l0[:64] = 1.0
    l1 = np.zeros((128, 128), np.float32)
    l1[64:] = 1.0
    c["lsel"] = np.ascontiguousarray(np.stack([l0, l1], axis=1))
    return c


PP = {}
_o = 0
for _n, _w in (("bout", 16), ("lng", 16), ("lnb", 16), ("psc", 4), ("subln", 1), ("convw", 36), ("convb", 12),
               ("gbs", 512), ("hb", 1024), ("lq", 256), ("fb1", 1), ("fb2", 1), ("ffr", 1), ("fw1", 64), ("fw2", 64)):
    PP[_n] = (_o, _w)
    _o += _w
NPP = _o


def _pack_params(inp, l):
    pp = np.zeros((128, NPP), np.float32)

    def put(name, arr):
        o, w = PP[name]
        pp[:arr.shape[0], o:o + w] = arr
    put("bout", inp["b_out"][l].reshape(16, 128).T)
    put("lng", inp["ln_g"][l].reshape(16, 128).T)
    put("lnb", inp["ln_b"][l].reshape(16, 128).T)
    put("psc", inp["pool_scale"][l].reshape(4, 128).T)
    put("subln", inp["subln_w"][l].reshape(128, 1))
    put("convw", inp["conv_w"][l].reshape(3, 12, 128).transpose(2, 0, 1).reshape(128, 36))
    put("convb", inp["conv_b"][l].reshape(12, 128).T)
    put("gbs", np.broadcast_to(inp["gmlp_b"][l].reshape(1, 512), (128, 512)))
    put("hb", np.broadcast_to(inp["hyena_bias"][l].reshape(1, 1024), (128, 1024)))
    put("lq", np.broadcast_to(inp["lambda_qk"][l].reshape(1, 256), (128, 256)))
    put("fb1", inp["filt_b1"][l].reshape(64, 1))
    put("fb2", inp["filt_b2"][l].reshape(64, 1))
    put("ffr", inp["filt_freq"][l].reshape(64, 1))
    put("fw1", inp["filt_w1"][l])
    put("fw2", inp["filt_w2"][l])
    return pp


class _Stop(Exception):
    pass


def build_program(debug=False, nlayers=DEPTH, limit=None, plan="auto"):
    if plan == "auto":
        plan = build_program(debug=debug, nlayers=nlayers, limit=limit, plan=None)._ring_plan
    return _build_program(debug, nlayers, limit, plan)


def _build_program(debug, nlayers, limit, plan):
    nc = bass.Bass("TRN2", target_bir_lowering=False)

    def din(name, shape, dt=F32):
        return nc.dram_tensor(name, list(shape), dt, kind="ExternalInput").ap()

    def dout(name, shape, dt=F32):
        return nc.dram_tensor(name, list(shape), dt, kind="ExternalOutput").ap()

    xT_d = din("xT", [DM, NT])
    cond_d = din("cond", [128, 16])
    ckT_d = din("ckT", [DEPTH, 512, 512])
    cv_d = din("cv", [DEPTH, 512, 512])
    wmod_d = din("w_mod", [DEPTH, DM, 3 * DM])
    bmod_d = din("b_modT", [DEPTH, 128, 48])
    win_d = din("w_in", [DEPTH, DM, NPIECE * 512])
    wout_d = din("w_out", [DEPTH, DM, DM])
    gw_d = din("gmlp_wT", [DEPTH, 128, 4, 128])
    pw_d = din("pool_w", [DEPTH, 128, 4, 128])
    w3_d = din("filt_w3", [DEPTH, 64, 2048])
    pp_d = din("pp", [DEPTH, 128, NPP])
    cos_d = din("cos", [128, NT])
    sin_d = din("sin", [128, NT])
    maskb_d = din("maskb", [128, 48])
    rcpad_d = din("rcpad", [4, 128, 1104])
    featsT_d = din("featsT", [64, NT])
    dec_d = din("dec", [NT, 512])
    smask_d = din("smask", [128, 24])
    wf_d = din("wf", [2, 128, 8, 1024], BF16)
    gi_d = din("gi", [2, 128, 16, 512], BF16)
    mint_d = din("mint", [128, 1])
    ident_d = din("ident", [128, 128])
    ones_d = din("ones", [128, 128])
    ropeT_d = din("ropeT", [128, 128])
    lsel_d = din("lsel", [128, 2, 128])

    yT_o = dout("yT_out", [DM, NT])
    kT_o = dout("kT_out", [DEPTH, 512, NT])
    v_o = dout("v_out", [DEPTH, NT, 512])
    xs_d = nc.dram_tensor("xspill", [DM, NT], F32, kind="Internal").ap()
    kp_d = nc.dram_tensor("kspec", [2, 8, 128, 1024], F32, kind="Internal").ap()
    dbg = {}
    if debug:
        dbg["hT"] = dout("dbg_hT", [128, 16 * NT], BF16)
        dbg["yT"] = dout("dbg_yT", [128, 16 * NT], BF16)
        dbg["mod"] = dout("dbg_mod", [128, 48])

    es = contextlib.ExitStack()
    with es:
        ARENA = 212000
        sb = es.enter_context(nc.sbuf_tensor("arena", [128, ARENA // 4], F32))
        ps = es.enter_context(nc.psum_tensor("psarena", [128, 4096], F32))
        sems = {e: es.enter_context(nc.semaphore("s_" + e)) for e in Sched.ENGS}
        dsems = {}
        S = Sched()
        sba = sb[:, :]
        psa = ps[:, :]

        cur = [0]

        def alloc(dtype, shape, pn=128):
            t = T(sba, "sb", cur[0], dtype, shape, 0, pn)
            cur[0] += (t.nbytes + 31) // 32 * 32
            return t

        def at(off, dtype, shape, pn=128):
            return T(sba, "sb", off, dtype, shape, 0, pn)

        banks = [T(psa, "ps", 2048 * i, F32, [512]) for i in range(8)]
        banks_bf = [T(psa, "ps", 2048 * i, BF16, [1024]) for i in range(8)]
        bctr = [0]

        reserved = set()

        def nbank():
            b = bctr[0] % 8
            bctr[0] += 1
            while b in reserved:
                b = bctr[0] % 8
                bctr[0] += 1
            return b

        def EN(e, name):
            return e

        POOL_TO_DVE = True

        def TT(eng, out, in0, in1, op):
            if POOL_TO_DVE and eng == "pool":
                eng = "dve"
            S.add(eng, lambda e: e.tensor_tensor(out=out.ap, in0=in0.ap, in1=in1.ap, op=op),
                  reads=[in0, in1], writes=[out])

        def TS(eng, out, in0, s1, s2, op0, op1=None):
            if POOL_TO_DVE and eng == "pool":
                eng = "dve"
            rd = [in0]
            a1 = s1.ap if isinstance(s1, Acc) else s1
            a2 = s2.ap if isinstance(s2, Acc) else s2
            if isinstance(s1, Acc):
                rd.append(s1)
            if isinstance(s2, Acc):
                rd.append(s2)
            if op1 is None:
                S.add(eng, lambda e: e.tensor_scalar(out=out.ap, in0=in0.ap, scalar1=a1, scalar2=None, op0=op0),
                      reads=rd, writes=[out])
            else:
                S.add(eng, lambda e: e.tensor_scalar(out=out.ap, in0=in0.ap, scalar1=a1, scalar2=a2, op0=op0, op1=op1),
                      reads=rd, writes=[out])

        def STT(eng, out, in0, sc, in1, op0, op1):
            eng = "dve"
            rd = [in0, in1]
            a = sc.ap if isinstance(sc, Acc) else sc
            if isinstance(sc, Acc):
                rd.append(sc)
            S.add(eng, lambda e: e.scalar_tensor_tensor(out=out.ap, in0=in0.ap, scalar=a, in1=in1.ap, op0=op0, op1=op1),
                  reads=rd, writes=[out])

        def ACT(out, in_, func, bias=0.0, scale=1.0):
            rd = [in_]
            b = bias.ap if isinstance(bias, Acc) else bias
            s = scale.ap if isinstance(scale, Acc) else scale
            if isinstance(bias, Acc):
                rd.append(bias)
            if isinstance(scale, Acc):
                rd.append(scale)
            S.add("act", lambda e: e.activation(out=out.ap, in_=in_.ap, func=func, bias=b, scale=s),
                  reads=rd, writes=[out])

        def CP(eng, out, in_):
            if eng == "act":
                S.add("act", lambda e: e.copy(out=out.ap, in_=in_.ap), reads=[in_], writes=[out])
            else:
                S.add(eng, lambda e: e.tensor_copy(out=out.ap, in_=in_.ap), reads=[in_], writes=[out])

        def MSET(eng, out, val):
            S.add(eng, lambda e: e.memset(out.ap, val), writes=[out])

        def MM(out, lhsT, rhs, start, stop):
            S.add("pe", lambda e: e.matmul(out.ap, lhsT=lhsT.ap, rhs=rhs.ap, start=start, stop=stop),
                  reads=[lhsT, rhs], writes=[out])

        def TR(out, in_, ident):
            S.add("pe", lambda e: e.transpose(out=out.ap, in_=in_.ap, identity=ident.ap),
                  reads=[in_, ident], writes=[out])

        def DMA(q, out, in_, sem, reads, writes):
            oa = out.ap if isinstance(out, Acc) else out
            ia = in_.ap if isinstance(in_, Acc) else in_
            S.add(q, lambda e: e.dma_start(out=oa, in_=ia), reads=reads, writes=writes, dma_sem=sem)

        def RECIP(eng, out, in_):
            S.add(eng, lambda e: e.reciprocal(out=out.ap, in_=in_.ap), reads=[in_], writes=[out])

        IDF = alloc(F32, [128])
        IDB = alloc(BF16, [128])
        ONF = alloc(F32, [128])
        ONB = alloc(BF16, [128])
        LSEL = alloc(BF16, [2, 128])
        LSELF = alloc(F32, [2, 128])
        ROPT = alloc(F32, [128])
        COS = alloc(F32, [NT])
        SIN = alloc(F32, [NT])
        MASKB = alloc(F32, [48])
        MINT = alloc(F32, [1])
        SMASK = alloc(F32, [24])
        CONDT = alloc(F32, [16])
        SCT = alloc(BF16, [16])
        MODT = alloc(F32, [48])
        S1T = alloc(F32, [16])
        GBT = alloc(F32, [16])
        PPS = alloc(F32, [NPP])
        WST = alloc(BF16, [4, 128])
        WPB = alloc(BF16, [4, 128])
        W3B = alloc(BF16, [2048], 64)
        FEAT = alloc(F32, [NT], 64)
        SMALL = alloc(F32, [128])
        BT = alloc(F32, [8, 48])
        STG = alloc(F32, [4, 512])
        RING = [alloc(BF16, [16, 512]), alloc(BF16, [16, 512])]
        HT = alloc(BF16, [16, NT])
        YT = alloc(BF16, [16, NT])
        RB0 = cur[0]
        RB_SIZE = ARENA - RB0
        assert RB_SIZE >= 71808, RB_SIZE
        XT = at(RB0, F32, [16, NT])

        def pp(name, p1=128):
            o, w = PP[name]
            return T(sba, "sb", PPS.off + 4 * o, F32, [w], 0, p1)

        MEAN = at(YT.off, F32, [NT])
        RSTD = at(YT.off + 4096, F32, [NT])
        NMR = at(YT.off + 8192, F32, [NT])

        ring_ctr = [0]

        rec_plan = []
        issued = [0]

        def ring_src(key):
            kind = key[0]
            if kind == "win":
                return wcols(win_d[key[1]], 512 * key[2]), (16, 512)
            if kind == "wmod":
                return wcols(wmod_d[key[1]], 512 * key[2]), (16, 512)
            if kind == "wout":
                return wcols(wout_d[key[1]], 512 * key[2]), (16, 512)
            if kind == "wf":
                return wf_d[key[1]], (8, 1024)
            if kind == "gi":
                return gi_d[key[1]], (16, 512)
            raise KeyError(key)

        def ring_issue(k, key):
            src_ap, shape = ring_src(key)
            s = k % 2
            slot = at(RING[s].off, BF16, list(shape))
            step = shape[0] // 2
            for i in range(2):
                dst = slot[:, i * step:(i + 1) * step, :]
                DMA("pool", dst, src_ap[:, i * step:(i + 1) * step, :], "ring%d%s" % (s, "ab"[i % 2]),
                    reads=[], writes=[dst])

        def ring_load(key):
            k = ring_ctr[0]
            ring_ctr[0] += 1
            rec_plan.append(key)
            if plan is None:
                ring_issue(k, key)
            else:
                assert plan[k] == key, (k, plan[k], key)
                while issued[0] <= min(k + 1, len(plan) - 1):
                    ring_issue(issued[0], plan[issued[0]])
                    issued[0] += 1
            _, shape = ring_src(key)
            return at(RING[k % 2].off, BF16, list(shape))

        def wcols(w_ap, c0, n=512):
            return w_ap[:, c0:c0 + n].rearrange("(kc p) c -> p kc c", p=128)

        ldc = [0]

        def ld(dst, src, sem=None, q="sp"):
            if sem is None:
                sem = "c0"
            DMA(q, dst, src, sem, reads=[], writes=[dst])

        ld(IDF.all(), ident_d)
        ld(ONF.all(), ones_d)
        ld(ROPT.all(), ropeT_d)
        ld(LSELF.all(), lsel_d)
        ld(COS.all(), cos_d)
        ld(SIN.all(), sin_d)
        ld(MASKB.all(), maskb_d)
        ld(MINT.all(), mint_d)
        ld(SMASK.all(), smask_d)
        ld(CONDT.all(), cond_d)
        ld(FEAT.all(), featsT_d)
        CP("dve", IDB.all(), IDF.all())
        CP("dve", ONB.all(), ONF.all())
        CP("dve", LSEL.all(), LSELF.all())
        ACT(SCT.all(), CONDT.all(), AF.Silu)
        for i in range(4):
            dst = XT[:, 4 * i:4 * i + 4, :]
            DMA("sp", dst, xT_d[512 * i:512 * i + 512, :].rearrange("(kc p) t -> p kc t", p=128), "x",
                reads=[], writes=[dst])

        def ln_stats(X):
            for half in range(2):
                hs = slice(512 * half, 512 * half + 512)
                bS, bQ = nbank(), nbank()
                for kc in range(KC):
                    sq = STG[:, kc % 2, :]
                    ACT(sq, X[:, kc, hs], AF.Square)
                    MM(banks[bS].all(), ONF.all(), X[:, kc, hs], kc == 0, kc == KC - 1)
                    MM(banks[bQ].all(), ONF.all(), sq, kc == 0, kc == KC - 1)
                mean = MEAN[:, hs]
                S.add("act", lambda e, o=mean, i=banks[bS].all(): e.mul(out=o.ap, in_=i.ap, mul=1.0 / DM),
                      reads=[banks[bS].all()], writes=[mean])
                t = STG[:, 2, :]
                TT("dve", t, mean, mean, ALU.mult)
                STT("dve", t, banks[bQ].all(), 1.0 / DM, t, ALU.mult, ALU.subtract)
                ACT(t, t, AF.Sqrt, bias=LN_EPS, scale=1.0)
                RECIP("dve", RSTD[:, hs], t)
                STT("dve", NMR[:, hs], mean, -1.0, RSTD[:, hs], ALU.mult, ALU.mult)

        def ln_apply(X, out_fn, scale_fn, bias_fn):
            for half in range(2):
                for kc in range(KC):
                    hs = slice(512 * half, 512 * half + 512)
                    t = STG[:, 2 * (kc % 2), :]
                    TT("dve", t, X[:, kc, hs], RSTD[:, hs], ALU.mult)
                    t2 = STG[:, 2 * (kc % 2) + 1, :]
                    TT("pool", t2, t, NMR[:, hs], ALU.add)
                    ACT(out_fn(kc, hs), t2, AF.Identity, bias=bias_fn(kc), scale=scale_fn(kc))

        def proj_fm(slot, evac):
            for ft in range(4):
                for half in range(2):
                    b = nbank()
                    for kc in range(KC):
                        MM(banks[b].all(), slot[:, kc, 128 * ft:128 * ft + 128],
                           HT[:, kc, 512 * half:512 * half + 512], kc == 0, kc == KC - 1)
                    evac(ft, half, banks[b])

        def proj_tm(slot, evac):
            for tt in range(8):
                b = nbank()
                for kc in range(KC):
                    MM(banks[b].all(), HT[:, kc, 128 * tt:128 * tt + 128], slot[:, kc, :], kc == 0, kc == KC - 1)
                evac(tt, banks[b])

        def piece(l, p):
            return ring_load(("win", l, p))

        sgc = [0]
        if debug:
            MSET("pool", YT.all(), 0.0)

        def ck(k):
            S.phase = "L%s.ck%s" % (S.__dict__.get("layer", 0), k)
            if limit is not None and k == limit:
                raise _Stop()

        try:
          ck(1)
          for l in range(nlayers):
              lam_init = 0.8 - 0.6 * math.exp(-0.3 * l)
              S.layer = l
              S.phase = "L%d.mod" % l
              S.group_sems.update(["c0", "x", "pl%d" % l, "pq%d" % l, "spill%d" % l])
              ld(PPS.all(), pp_d[l], "pl%d" % l)
              DMA("pool", WST.all(), gw_d[l], "pq%d" % l, reads=[], writes=[WST.all()])
              DMA("pool", WPB.all(), pw_d[l], "pq%d" % l, reads=[], writes=[WPB.all()])
              DMA("pool", W3B.all(), w3_d[l], "pq%d" % l, reads=[], writes=[W3B.all()])
              BOUT, LNG, LNB, PSC = pp("bout"), pp("lng"), pp("lnb"), pp("psc")
              CONVW, CONVB = pp("convw"), pp("convb")
              GBS, HB, LQ = pp("gbs"), pp("hb"), pp("lq")

              BMT = at(YT.off + 28672, F32, [48])
              ld(BMT.all(), bmod_d[l], "pl%d" % l)
              AB2 = [at(HT.off, BF16, [2, 8, 512]), at(HT.off + 16384, BF16, [2, 8, 512])]
              FT = at(YT.off, F32, [8, 512])
              DECB = [at(YT.off + 16384, F32, [512]), at(YT.off + 18432, F32, [512])]
              KST = [at(YT.off + 20480, F32, [2, 512]), at(YT.off + 24576, F32, [2, 512])]
              H1 = at(STG.off + 4096, F32, [NT], 64)
              H2B = at(STG.off, BF16, [NT], 64)
              MR = at(STG.off + 2048, F32, [512], 1)

              def filter_chunks():
                  f2p = SMALL[0:64, 8:9]
                  c1 = SMALL[0:64, 9:10]
                  c2 = SMALL[0:64, 10:11]
                  TS("dve", f2p, pp("ffr", 64)[:, 0:1], 1.0 / TWO_PI, None, ALU.mult)
                  STT("dve", c1, pp("fb1", 64)[:, 0:1], 1.0, f2p, ALU.mult, ALU.mult)
                  STT("dve", c2, pp("fb2", 64)[:, 0:1], 1.0, f2p, ALU.mult, ALU.mult)
                  FW1 = pp("fw1", 33)
                  FW2 = pp("fw2", 64)
                  for (W, src_fn, cc, dst) in ((FW1, lambda hs: FEAT[0:33, hs], c1, H1), (FW2, lambda hs: H1[:, hs], c2, H2B)):
                      for half in range(2):
                          hs = slice(512 * half, 512 * half + 512)
                          b = nbank()
                          MM(banks[b][0:64, :], W.all(), src_fn(hs), True, True)
                          u = at(FT.off + 2048 * half, F32, [512], 64)
                          rn = at(FT.off + 4096 + 2048 * half, F32, [512], 64)
                          TS("dve", u.all(), banks[b][0:64, :], f2p, cc, ALU.mult, ALU.add)
                          TS("dve", rn.all(), u.all(), 12582912.0, 12582912.0, ALU.add, ALU.subtract)
                          TT("dve", u.all(), u.all(), rn.all(), ALU.subtract)
                          ACT(dst[:, hs], u.all(), AF.Sin, bias=0.0, scale=6.28318)
                      yield
                  decc = [0]

                  def dec_tile(tt):
                      k = decc[0] % 2
                      decc[0] += 1
                      DMA("sp", DECB[k].all(), dec_d[128 * tt:128 * tt + 128, :], "dec%d" % k, reads=[], writes=[DECB[k].all()])
                      return DECB[k]
                  for o in range(2):
                      bsum = nbank()
                      reserved.add(bsum)
                      def sums(tq):
                          for dr in range(2):
                              n_ = 2 * tq + dr
                              MM(banks[bsum].all(), ONF.all(), FT[:, 2 * (tq % 2) + dr, :], n_ == 0, n_ == 15)
                      for tt in range(8):
                          dct = dec_tile(tt)
                          for dr in range(2):
                              b = nbank()
                              cb = (2 * o + dr) * 512
                              MM(banks[b].all(), H2B[0:64, 128 * tt:128 * tt + 128], W3B[0:64, cb:cb + 512], True, True)
                              f = FT[:, 2 * (tt % 2) + dr, :]
                              STT("dve", f, banks[b].all(), SMASK[:, 8 * dr + tt:8 * dr + tt + 1], dct.all(), ALU.mult, ALU.mult)
                              ACT(f, f, AF.Abs)
                          if tt >= 1:
                              sums(tt - 1)
                          yield
                      sums(7)
                      rsum = FT[:, 6, :]
                      RECIP("dve", rsum, banks[bsum].all())
                      reserved.discard(bsum)
                      for tt in range(8):
                          dct = dec_tile(tt)
                          bF, bB = nbank(), nbank()
                          MM(banks[bF].all(), H2B[0:64, 128 * tt:128 * tt + 128], W3B[0:64, (2 * o) * 512:(2 * o) * 512 + 512], True, True)
                          MM(banks[bB].all(), H2B[0:64, 128 * tt:128 * tt + 128], W3B[0:64, (2 * o + 1) * 512:(2 * o + 1) * 512 + 512], True, True)
                          if tt % 2 == 0:
                              f, bw, sm_ = FT[:, 0, :], FT[:, 1, :], FT[:, 2, :]
                          else:
                              f, bw, sm_ = FT[:, 4, :], FT[:, 5, :], FT[:, 7, :]
                          TT("dve", f, banks[bF].all(), dct.all(), ALU.mult)
                          STT("dve", bw, banks[bB].all(), SMASK[:, 16 + tt:16 + tt + 1], dct.all(), ALU.mult, ALU.mult)
                          TT("dve", sm_, f, bw, ALU.add)
                          TT("dve", AB2[o][:, 0, tt, :], sm_, rsum, ALU.mult)
                          TT("dve", sm_, f, bw, ALU.subtract)
                          TT("dve", AB2[o][:, 1, tt, :], sm_, rsum, ALU.mult)
                          yield

              kstc = [0]

              WFS = at(YT.off, BF16, [8, 1024])

              def wfs_load(s_):
                  for i in range(2):
                      dst = WFS[:, 4 * i:4 * i + 4, :]
                      DMA("sp", dst, wf_d[s_][:, 4 * i:4 * i + 4, :], "wfs%s" % "ab"[i], reads=[], writes=[dst])

              def kspec_pair(i):
                  pl = i % 4
                  for o in range(2):
                      bKr, bKs = nbank(), nbank()
                      for kc in range(8):
                          MM(banks[bKr].all(), WFS[:, kc, 256 * pl:256 * pl + 128], AB2[o][:, 0, kc, :], kc == 0, kc == 7)
                          MM(banks[bKs].all(), WFS[:, kc, 256 * pl + 128:256 * pl + 256], AB2[o][:, 1, kc, :], kc == 0, kc == 7)
                      k = kstc[0] % 2
                      kstc[0] += 1
                      TT("dve", KST[k][:, 0, :], banks[bKr].all(), HB[:, 512 * o:512 * o + 512], ALU.add)
                      CP("act", KST[k][:, 1, :], banks[bKs].all())
                      DMA("sp", kp_d[o, i], KST[k].all(), "kst%d" % k, reads=[KST[k].all()],
                          writes=[dacc(kp_d, "kspec", 8 * o + i, 8 * o + i + 1)])

              gen = filter_chunks()
              bT = 7
              reserved.add(bT)

              def mod_slot(s_):
                  slot = ring_load(("wmod", l, s_))
                  b = nbank()
                  for kc in range(KC):
                      MM(banks[b][0:1, :], SCT[:, kc:kc + 1], slot[:, kc, :], kc == 0, kc == KC - 1)
                  CP("act", MR[0:1, :], banks[b][0:1, :])
                  for j in range(4):
                      MM(banks[bT][:, 4 * s_ + j:4 * s_ + j + 1], MR[0:1, 128 * j:128 * j + 128], ONF[0:1, 0:1], True, True)

              for s_ in range(12):
                  mod_slot(s_)
                  if s_ < 4:
                      for _ in range(13):
                          next(gen, None)
                  if s_ == 3:
                      for _ in gen:
                          pass
                      wfs_load(0)
                  if 4 <= s_ < 8:
                      kspec_pair(s_ - 4)
                      if s_ == 7:
                          wfs_load(1)
                  if s_ >= 8:
                      kspec_pair(s_ - 4)
              TT("dve", MODT.all(), banks[bT][:, 0:48], BMT.all(), ALU.add)
              reserved.discard(bT)
              SHIFT = MODT
              TS("dve", S1T.all(), MODT[:, 16:32], 1.0, None, ALU.add)
              TT("dve", GBT.all(), MODT[:, 32:48], BOUT.all(), ALU.mult)
              if debug and l == 0:
                  DMA("sp", dbg["mod"], MODT.all(), "out", reads=[MODT.all()], writes=[dacc(dbg["mod"], "dbg_mod")])

              ck(2)
              lam_t = SMALL[:, 0:1]
              negl = SMALL[:, 1:2]
              swv = SMALL[:, 2:3]
              tmpq = SMALL[:, 16:80]
              e1 = SMALL[:, 3:4]
              e2 = SMALL[:, 4:5]
              TT("dve", tmpq, LQ[:, 0:64], LQ[:, 64:128], ALU.mult)
              S.add("dve", lambda e, o=e1, i=tmpq: e.reduce_sum(out=o.ap, in_=i.ap, axis=AX.X), reads=[tmpq], writes=[e1])
              TT("dve", tmpq, LQ[:, 128:192], LQ[:, 192:256], ALU.mult)
              S.add("dve", lambda e, o=e2, i=tmpq: e.reduce_sum(out=o.ap, in_=i.ap, axis=AX.X), reads=[tmpq], writes=[e2])
              ACT(e1, e1, AF.Exp)
              ACT(e2, e2, AF.Exp)
              TT("dve", lam_t, e1, e2, ALU.subtract)
              TS("dve", negl, lam_t, lam_init, -1.0, ALU.add, ALU.mult)
              TS("dve", swv, pp("subln")[:, 0:1], 1.0 - lam_init, None, ALU.mult)

              ln_stats(XT)
              ln_apply(XT, lambda kc, hs: HT[:, kc, hs], lambda kc: S1T[:, kc:kc + 1], lambda kc: MODT[:, kc:kc + 1])
              if debug and l == 0:
                  DMA("sp", dbg["hT"], HT.all(), "out", reads=[HT.all()], writes=[dacc(dbg["hT"], "dbg_hT")])
              for i in range(4):
                  src = XT[:, 4 * i:4 * i + 4, :]
                  DMA("sp", xs_d[512 * i:512 * i + 512, :].rearrange("(kc p) t -> p kc t", p=128), src, "spill%d" % l,
                      reads=[src], writes=[dacc(xs_d, "xspill", i, i + 1)])

              ck(3)
              R0 = RB0

              def sgbuf():
                  sgc[0] += 1
                  return at(SGOFF + 2048 * (sgc[0] % 2), F32, [512])

              VLN = at(R0, BF16, [8, 512])
              MIX = at(R0 + 8192, F32, [4, NT])
              SGOFF = R0 + 8192 + 16384
              slot = piece(l, 1)

              def evac_av(tt, bank):
                  st6 = SMALL[:, 80:86]
                  mv = SMALL[:, 86:88]
                  rs = SMALL[:, 88:89]
                  S.add("dve", lambda e, o=st6, i=bank.all(): e.bn_stats(out=o.ap, in_=i.ap), reads=[bank.all()], writes=[st6])
                  S.add("dve", lambda e, o=mv, i=st6: e.bn_aggr(out=o.ap, in_=i.ap), reads=[st6], writes=[mv])
                  ACT(rs, SMALL[:, 87:88], AF.Sqrt, bias=LN_EPS, scale=1.0)
                  RECIP("dve", rs, rs)
                  TS("dve", VLN[:, tt, :], bank.all(), SMALL[:, 86:87], rs, ALU.subtract, ALU.mult)
              proj_tm(slot, evac_av)
              for g in range(4):
                  bb = [nbank(), nbank()]
                  for tt in range(8):
                      MM(banks[bb[tt // 4]][:, 128 * (tt % 4):128 * (tt % 4) + 128], VLN[:, tt, 128 * g:128 * g + 128],
                         WST[:, g, :], True, True)
                  for half in range(2):
                      for j in range(4):
                          TT("dve", MIX[:, g, 512 * half + 128 * j:512 * half + 128 * j + 128],
                             banks[bb[half]][:, 128 * j:128 * j + 128], GBS[:, 128 * g:128 * g + 128], ALU.add)
              slot = piece(l, 0)
              proj_fm(slot, lambda g, half, bank: TT("dve", MIX[:, g, 512 * half:512 * half + 512], bank.all(),
                                                     MIX[:, g, 512 * half:512 * half + 512], ALU.mult))
              slot = piece(l, 2)

              def evac_gate(ybase, SRC):
                  def f(g, half, bank):
                      sg = sgbuf()
                      ACT(sg.all(), bank.all(), AF.Silu)
                      TT("pool", YT[:, ybase + g, 512 * half:512 * half + 512], sg.all(),
                         SRC[:, g, 512 * half:512 * half + 512], ALU.mult)
                  return f
              proj_fm(slot, evac_gate(0, MIX))

              ck(4)
              PB = at(R0, F32, [1104])
              PBseg = at(R0 + 64, F32, [4, 272])
              SA = at(R0 + 4480, F32, [1104])
              SAseg = at(R0 + 4480 + 64, F32, [4, 272])
              SBf = at(R0 + 8960, F32, [1104])
              SBseg = at(R0 + 8960 + 64, F32, [4, 272])
              RCP = at(R0 + 13440, F32, [1104])
              DIFFS = [at(R0 + 17920 + 2048 * q, BF16, [4, 256]) for q in range(4)]
              DIFFLS = [at(R0 + 17920 + 2048 * q, BF16, [NT]) for q in range(4)]
              POOLED = at(R0 + 26624, F32, [4, NT])
              SGOFF = R0 + 26624 + 16384

              def pooled_mm(g):
                  for hh in range(2):
                      b = nbank()
                      MM(banks[b].all(), WPB[:, g, :], DIFFLS[g][:, 512 * hh:512 * hh + 512], True, True)
                      ACT(POOLED[:, g, 512 * hh:512 * hh + 512], banks[b].all(), AF.Identity, bias=0.0, scale=PSC[:, g:g + 1])
              MSET("pool", SA.all(), 0.0)
              MSET("pool", SBf.all(), 0.0)
              slot = piece(l, 3)

              def evac_bx(g, half, bank):
                  if half == 0:
                      MSET("pool", PB.all(), 0.0)
                  for sl in range(2):
                      s = 2 * half + sl
                      CP("act", PB[:, 16 + 272 * s:16 + 272 * s + 256], bank[:, 256 * sl:256 * sl + 256])
                  if half == 1:
                      for s in range(1, 4):
                          TS("pool", PB[:, 8 + 272 * s:16 + 272 * s], PB[:, 16 + 272 * (s - 1) + 248:16 + 272 * (s - 1) + 256],
                             MINT[:, 0:1], None, ALU.mult)
                      for s in range(0, 3):
                          TS("pool", PB[:, 8 + 272 * s + 264:8 + 272 * s + 272], PB[:, 16 + 272 * (s + 1):16 + 272 * (s + 1) + 8],
                             MINT[:, 0:1], None, ALU.mult)
                      lo, hi = 8, 1096
                      TT("pool", SA[:, lo:hi], PB[:, lo - 1:hi - 1], PB[:, lo:hi], ALU.add)
                      src, dst, sseg, dseg = SA, SBf, SAseg, SBseg
                      for k in range(g):
                          sh = 1 << k
                          TT("dve" if k % 2 == 0 else "pool", dst[:, lo:hi], src[:, lo - sh:hi - sh], src[:, lo + sh:hi + sh], ALU.add)
                          src, dst, sseg, dseg = dst, src, dseg, sseg
                      DMA("sp", RCP.all(), rcpad_d[g], "misc", reads=[], writes=[RCP.all()])
                      TT("pool", dst[:, lo:hi], src[:, lo:hi], RCP[:, lo:hi], ALU.mult)
                      TT("dve", DIFFS[g].all(), dseg[:, :, 0:256], PBseg[:, :, 0:256], ALU.subtract)
              proj_fm(slot, evac_bx)
              for g_ in range(4):
                  pooled_mm(g_)
              slot = piece(l, 4)
              proj_fm(slot, evac_gate(4, POOLED))

              ck(5)
              QT = at(R0, BF16, [4, NT])
              KT = at(R0 + 8192, BF16, [4, 1536])
              VB = at(R0 + 20480, BF16, [12, 512])
              O32 = at(R0 + 32768, F32, [4, NT])
              KRAW = O32
              TMP = at(R0 + 49152, F32, [8, 512])
              PTB = at(R0 + 65536, BF16, [4, 512])
              SGOFF = R0 + 49152 + 8192
              CST = at(TMP.off + 8192, F32, [4, 512])
              DMA("sp", CST.all(), ckT_d[l].rearrange("(h p) k -> p h k", p=128), "ck", reads=[], writes=[CST.all()])
              for h in range(4):
                  CP("dve" if h % 2 else "act", KT[:, h, 0:512], CST[:, h, :])
              DMA("sp", CST.all(), cv_d[l].rearrange("(t p) c -> p t c", p=128), "ck", reads=[], writes=[CST.all()])
              for h in range(4):
                  CP("dve" if h % 2 else "act", VB[:, h, :], CST[:, h, :])

              pend = []

              def rope_finish():
                  if pend:
                      q32, hs, dst = pend.pop(0)
                      b = nbank()
                      MM(banks[b].all(), ROPT.all(), q32, True, True)
                      t1 = TMP[:, 2, :]
                      t2 = TMP[:, 3, :]
                      TT("dve", t1, q32, COS[:, hs], ALU.mult)
                      TT("dve", t2, banks[b].all(), SIN[:, hs], ALU.mult)
                      TT("pool", dst, t1, t2, ALU.add)

              def rope_evac(dst_fn, raw_fn):
                  def f(h, half, bank):
                      hs = slice(512 * half, 512 * half + 512)
                      q32 = raw_fn(h, half, hs)
                      CP("act", q32, bank.all())
                      rope_finish()
                      pend.append((q32, hs, dst_fn(h, hs)))
                  return f
              slot = piece(l, 5)
              proj_fm(slot, rope_evac(lambda h, hs: QT[:, h, hs], lambda h, half, hs: TMP[:, half, :]))
              slot = piece(l, 6)
              proj_fm(slot, rope_evac(lambda h, hs: KT[:, h, 512 + hs.start:512 + hs.stop], lambda h, half, hs: KRAW[:, h, hs]))
              rope_finish()
              DMA("sp", kT_o[l].rearrange("(h p) t -> p h t", p=128), KRAW.all(), "kout", reads=[KRAW.all()],
                  writes=[dacc(kT_o, "kT_out")])
              ck(51)
              slot = piece(l, 7)

              def evac_v(tt, bank):
                  import os
                  st = STG[:, tt % 4, :]
                  if os.environ.get("DBG_V", "") != "noact":
                      CP("act", st, bank.all())
                  if os.environ.get("DBG_V", "") not in ("nodma", "noact"):
                      DMA("sp", v_o[l, 128 * tt:128 * tt + 128, :], st, "vout%d" % (tt % 4), reads=[st], writes=[dacc(v_o, "v_out", tt, tt + 1)])
                  CP("pool", VB[:, 4 + tt, :], st)
              proj_tm(slot, evac_v)
              ck(52)
              MX = SMALL[:, 96:128]
              for h in range(4):
                  col = 0
                  blocks = [(QT, h, 512 * i) for i in range(2)] + [(KT, h, 512 * i) for i in range(3)]
                  for (src, hh, c0) in blocks:
                      sq = PTB[:, col % 4, :]
                      TT("pool", sq, src[:, hh, c0:c0 + 512], src[:, hh, c0:c0 + 512], ALU.mult)
                      for m in range(2):
                          b = nbank()
                          MM(banks[b].all(), LSEL[:, m, :], sq, True, True)
                          o = SMALL[:, 96 + m * 8 + col:96 + m * 8 + col + 1]
                          S.add("dve", lambda e, o=o, i=banks[b].all(): e.reduce_max(out=o.ap, in_=i.ap, axis=AX.X),
                                reads=[banks[b].all()], writes=[o])
                      col += 1
                  for m in range(2):
                      base = 96 + m * 8
                      qm = SMALL[:, base + 5:base + 6]
                      km = SMALL[:, base + 6:base + 7]
                      TT("dve", qm, SMALL[:, base:base + 1], SMALL[:, base + 1:base + 2], ALU.max)
                      TT("dve", km, SMALL[:, base + 2:base + 3], SMALL[:, base + 3:base + 4], ALU.max)
                      TT("dve", km, km, SMALL[:, base + 4:base + 5], ALU.max)
                      TT("dve", qm, qm, km, ALU.mult)
                      ACT(qm, qm, AF.Sqrt, bias=0.0, scale=(1.02 / 8.0) ** 2)
                      TS("dve", qm, qm, -1.0, None, ALU.mult)
                      TS("dve", BT[:, 2 * h + m, :], MASKB.all(), qm, None, ALU.add)
              ck(53)
              groups = [(h, half) for h in range(4) for half in range(2)]
              steps = [(g, m, kt) for g in range(8) for m in range(2) for kt in range(12)]
              OB = [3, 4]
              SBK = [5, 6]
              SCB = [0, 1, 2, 7]

              QZ = at(TMP.off + 8192, BF16, [4, 512])
              MSET("pool", QZ.all(), 0.0)
              SACC = [TMP[:, 6, :], TMP[:, 7, :]]
              SACCP = [STG[:, 0, :], STG[:, 1, :]]

              def emit_qk(i):
                  g, m, kt = steps[i]
                  h, half = groups[g]
                  hs = slice(512 * half, 512 * half + 512)
                  qz = 2 * (g % 2) + m
                  if kt == 0:
                      CP("dve", QZ[64 * m:64 * m + 64, qz, :], QT[64 * m:64 * m + 64, h, hs])
                  MM(banks[SCB[i % 4]].all(), KT[:, h, 128 * kt:128 * kt + 128], QZ[:, qz, :], True, True)

              def emit_exp_pv(i):
                  g, m, kt = steps[i]
                  h, half = groups[g]
                  pk = i % 4
                  for sl in range(2):
                      sg_ = 2 * half + sl
                      ACT(PTB[:, pk, 256 * sl:256 * sl + 256], banks[SCB[i % 4]][:, 256 * sl:256 * sl + 256], AF.Exp,
                          bias=BT[:, 2 * h + m, kt * 4 + sg_:kt * 4 + sg_ + 1], scale=0.125)
                  MM(banks[OB[m]].all(), VB[:, kt, 128 * h:128 * h + 128], PTB[:, pk, :], kt == 0, kt == 11)
                  pt_ = PTB[:, pk, :]
                  if kt == 0:
                      CP("dve", SACC[m], pt_)
                  elif kt == 1:
                      CP("pool", SACCP[m], pt_)
                  elif kt % 2 == 0:
                      TT("dve", SACC[m], SACC[m], pt_, ALU.add)
                  else:
                      acc_ = SACCP[m]
                      S.add("pool", lambda e, o=acc_, a=acc_, b_=pt_: e.tensor_tensor(out=o.ap, in0=a.ap, in1=b_.ap, op=ALU.add),
                            reads=[acc_, pt_], writes=[acc_])
                  if kt == 11:
                      MM(banks[SBK[m]].all(), ONF.all(), SACC[m], True, False)
                      MM(banks[SBK[m]].all(), ONF.all(), SACCP[m], False, True)

              def combine1(g):
                  h, half = groups[g]
                  hs = slice(512 * half, 512 * half + 512)
                  c = [TMP[:, k, :] for k in range(4)]
                  CP("dve", c[0], banks[OB[0]].all())
                  CP("dve", c[1], banks[OB[1]].all())
                  ACT(c[2], banks[SBK[0]].all(), AF.Ln)
                  ACT(c[3], banks[SBK[1]].all(), AF.Ln)
                  ACT(c[2], c[2], AF.Exp, bias=0.0, scale=-1.0)
                  ACT(c[3], c[3], AF.Exp, bias=0.0, scale=-1.0)
                  TT("dve", c[0], c[0], c[2], ALU.mult)
                  TT("pool", c[1], c[1], c[3], ALU.mult)
                  STT("dve", O32[:, h, hs], c[1], negl, c[0], ALU.mult, ALU.add)
                  TT("pool", c[2], O32[:, h, hs], O32[:, h, hs], ALU.mult)

              def combine2(g):
                  h, half = groups[g]
                  hs = slice(512 * half, 512 * half + 512)
                  r = TMP[:, 3, :]
                  MM(banks[5].all(), ONF.all(), TMP[:, 2, :], True, True)
                  ACT(r, banks[5].all(), AF.Ln, bias=1e-5, scale=1.0 / 128.0)
                  ACT(r, r, AF.Exp, bias=0.0, scale=-0.5)
                  STT("dve", O32[:, h, hs], O32[:, h, hs], swv, r, ALU.mult, ALU.mult)

              NS = len(steps)
              for i in range(NS + 3):
                  if i < NS:
                      emit_qk(i)
                  j = i - 3
                  if j >= 0:
                      emit_exp_pv(j)
                      g, m, kt = steps[j]
                      if m == 1 and kt == 11:
                          combine1(g)
                      if m == 0 and kt == 8 and g >= 1:
                          combine2(g - 1)
              combine2(7)
              slot = piece(l, 8)
              proj_fm(slot, evac_gate(8, O32))

              ck(6)
              G = at(R0, F32, [4, NT])
              Gseg = at(R0, F32, [4, 4, 256])
              ZT = at(R0 + 16384, BF16, [8, 512])
              Y = at(R0 + 24576, BF16, [16, 512])
              KP = [at(R0 + 40960, F32, [2, 512]), at(R0 + 45056, F32, [2, 512])]
              MTS = [at(R0 + 49152, F32, [4, 512]), at(R0 + 57344, F32, [4, 512])]
              PC = at(R0 + 49152, F32, [1040])
              PCp = at(R0 + 49152 + 12, F32, [4, 258])
              ACC = at(R0 + 49152 + 4160, F32, [1040])
              ACCs = at(R0 + 49152 + 4160 + 8, F32, [4, 258])
              SGOFF = R0 + 57344 + 4096

              def conv3(p, cbase):
                  S.phase = "L%d.D.conv3" % l
                  slot = piece(l, p)
                  MSET("pool", PC.all(), 0.0)

                  def ev(ft, half, bank):
                      for sl in range(2):
                          s = 2 * half + sl
                          CP("act", PC[:, 2 + 258 * s:2 + 258 * s + 256], bank[:, 256 * sl:256 * sl + 256])
                      if half == 1:
                          for s in range(1, 4):
                              TS("pool", PC[:, 1 + 258 * s:2 + 258 * s], PC[:, 2 + 258 * (s - 1) + 255:2 + 258 * (s - 1) + 256],
                                 MINT[:, 0:1], None, ALU.mult)
                          for s in range(0, 3):
                              TS("pool", PC[:, 258 * s + 258:258 * s + 259], PC[:, 2 + 258 * (s + 1):3 + 258 * (s + 1)],
                                 MINT[:, 0:1], None, ALU.mult)
                          ci = cbase + ft
                          ACT(ACC[:, 1:1033], PC[:, 1:1033], AF.Identity, bias=CONVB[:, ci:ci + 1], scale=CONVW[:, 12 + ci:13 + ci])
                          STT("dve", ACC[:, 1:1033], PC[:, 0:1032], CONVW[:, ci:ci + 1], ACC[:, 1:1033], ALU.mult, ALU.add)
                          STT("pool", Gseg[:, ft, :, :], PCp[:, :, 0:256], CONVW[:, 24 + ci:25 + ci], ACCs[:, :, 0:256], ALU.mult, ALU.add)
                  proj_fm(slot, ev)

              def to_time_major():
                  S.phase = "L%d.D.tr" % l
                  for ft in range(4):
                      for hh in range(2):
                          b = nbank()
                          for j in range(4):
                              tt = 4 * hh + j
                              TR(banks[b][:, 128 * j:128 * j + 128], G[:, ft, 128 * tt:128 * tt + 128], IDF.all())
                          src = T(psa, "ps", 2048 * b, F32, [4, 128])
                          CP("act" if hh == 0 else "dve", ZT[:, 4 * hh:4 * hh + 4, 128 * ft:128 * ft + 128], src.all())

              def long_conv(o):
                  S.phase = "L%d.D.fwd" % l
                  for s in range(2):
                      wslot = ring_load(("wf", s))
                      for pl in range(4):
                          i = 4 * s + pl
                          k = i % 2
                          DMA("sp", KP[k].all(), kp_d[o, i], "kp%d" % k,
                              reads=[dacc(kp_d, "kspec", 8 * o + i, 8 * o + i + 1)], writes=[KP[k].all()])
                          bZr, bZs = nbank(), nbank()
                          for kc in range(8):
                              MM(banks[bZr].all(), wslot[:, kc, 256 * pl:256 * pl + 128], ZT[:, kc, :], kc == 0, kc == 7)
                              MM(banks[bZs].all(), wslot[:, kc, 256 * pl + 128:256 * pl + 256], ZT[:, kc, :], kc == 0, kc == 7)
                          kre, ks = KP[k][:, 0, :], KP[k][:, 1, :]
                          zre, zs, t1, t2 = (MTS[k][:, q, :] for q in range(4))
                          CP("act", zre, banks[bZr].all())
                          CP("act", zs, banks[bZs].all())
                          TT("dve", t1, zre, kre, ALU.mult)
                          TT("pool", t2, zs, ks, ALU.mult)
                          TT("dve", Y[:, i, :], t1, t2, ALU.subtract)
                          TT("dve", zre, zre, ks, ALU.mult)
                          TT("pool", zs, zs, kre, ALU.mult)
                          TT("dve", Y[:, 8 + i, :], zre, zs, ALU.add)
                  S.phase = "L%d.D.inv" % l
                  for half in range(2):
                      gslot = ring_load(("gi", half))
                      for ft in range(4):
                          b = nbank()
                          for kc in range(16):
                              MM(banks[b].all(), Y[:, kc, 128 * ft:128 * ft + 128], gslot[:, kc, :], kc == 0, kc == 15)
                          TT("dve", G[:, ft, 512 * half:512 * half + 512], banks[b].all(), G[:, ft, 512 * half:512 * half + 512], ALU.mult)

              conv3(11, 8)
              to_time_major()
              conv3(9, 0)
              long_conv(0)
              to_time_major()
              conv3(10, 4)
              long_conv(1)
              slot = piece(l, 12)
              proj_fm(slot, evac_gate(12, G))

              ck(7)
              XR = [at(STG.off, F32, [NT]), at(STG.off + 4096, F32, [NT])]
              ET = [at(HT.off, F32, [512]), at(HT.off + 2048, F32, [512])]
              ec = 0
              for s in range(4):
                  slot = ring_load(("wout", l, s))
                  for j in range(4):
                      jt = 4 * s + j
                      xr = XR[jt % 2]
                      DMA("sp", xr.all(), xs_d[128 * jt:128 * jt + 128, :], "xr%d" % (jt % 2),
                          reads=[dacc(xs_d, "xspill", jt // 4, jt // 4 + 1)], writes=[xr.all()])
                      for half in range(2):
                          hs = slice(512 * half, 512 * half + 512)
                          b = nbank()
                          for kc in range(KC):
                              MM(banks[b].all(), slot[:, kc, 128 * j:128 * j + 128], YT[:, kc, hs], kc == 0, kc == KC - 1)
                          et = ET[ec % 2]
                          ec += 1
                          ACT(et.all(), banks[b].all(), AF.Identity, bias=GBT[:, jt:jt + 1], scale=MODT[:, 32 + jt:33 + jt])
                          STT("dve", XT[:, jt, hs], xr[:, hs], ALPHA, et.all(), ALU.mult, ALU.add)
              S.phase = "L%d.E.ln" % l
              ln_stats(XT)
              ln_apply(XT, lambda kc, hs: XT[:, kc, hs], lambda kc: LNG[:, kc:kc + 1], lambda kc: LNB[:, kc:kc + 1])


        except _Stop:
            pass
        if debug:
            DMA("sp", dbg["yT"], YT.all(), "out", reads=[YT.all()], writes=[dacc(dbg["yT"], "dbg_yT")])
        for i in range(4):
            src = XT[:, 4 * i:4 * i + 4, :]
            DMA("sp", yT_o[512 * i:512 * i + 512, :].rearrange("(kc p) t -> p kc t", p=128), src, "out",
                reads=[src], writes=[dacc(yT_o, "yT_out")])

        for n in sorted(S.dma_counts):
            dsems[n] = es.enter_context(nc.semaphore("d_" + n))
        with nc.Block() as block:
            @block.tensor
            def _(eng):
                S.emit_one("pe", eng, sems, dsems)

            @block.scalar
            def _(eng):
                S.emit_one("act", eng, sems, dsems)

            @block.vector
            def _(eng):
                S.emit_one("dve", eng, sems, dsems)

            @block.gpsimd
            def _(eng):
                S.emit_one("pool", eng, sems, dsems)

            @block.sync
            def _(eng):
                S.emit_one("sp", eng, sems, dsems, final=True)
    nc._sched = S
    nc._ring_plan = rec_plan
    return nc


_CACHE = {}


def make_in_maps(inp):
    f32 = lambda a: np.ascontiguousarray(np.asarray(a, dtype=np.float32))
    inp = {k: np.asarray(v) for k, v in inp.items()}
    fixed = _fixed_consts()
    tabs = {"s": _tables("s"), "p": _tables("p")}
    shared = {
        "w_mod": f32(inp["w_mod"]),
        "b_modT": f32(np.transpose(inp["b_mod"].reshape(DEPTH, 48, 128), (0, 2, 1))),
        "w_in": f32(inp["w_in"]),
        "w_out": f32(inp["w_out"]),
        "gmlp_wT": f32(np.transpose(inp["gmlp_w"], (0, 3, 1, 2))),
        "pool_w": f32(np.transpose(inp["pool_w"], (0, 2, 1, 3))),
        "filt_w3": f32(inp["filt_w3"]),
        "pp": f32(np.stack([_pack_params(inp, l) for l in range(DEPTH)])),
    }
    shared.update(fixed)
    maps = []
    for core in range(8):
        m = dict(shared)
        if core < 4:
            b = core
            kind = "s"
            x = inp["x_sample"][b]
            cond = inp["c"][b]
            ck = inp["cache_k"][b].reshape(DEPTH, 512, 512)
            cv = inp["cache_v"][b].reshape(DEPTH, 512, 512)
        else:
            kind = "p"
            x = inp["x_prompt"][4 * (core - 4):4 * (core - 4) + 4].reshape(NT, DM)
            cond = inp["c_ctx"]
            ck = np.zeros((DEPTH, 512, 512), np.float32)
            cv = np.zeros((DEPTH, 512, 512), np.float32)
        m["xT"] = f32(x.T)
        m["cond"] = f32(cond.reshape(16, 128).T)
        m["ckT"] = f32(np.transpose(ck, (0, 2, 1)))
        m["cv"] = f32(cv)
        for k, v in tabs[kind].items():
            m[k] = v
        maps.append(m)
    return maps


def kernel(**inputs):
    if "nc" not in _CACHE:
        _CACHE["nc"] = build_program()
    nc = _CACHE["nc"]
    maps = make_in_maps(inputs)
    res = run_bass_kernel_spmd(nc, maps, core_ids=list(range(8)))
    r = res.results
    y_sample = np.stack([np.ascontiguousarray(r[c]["yT_out"].T) for c in range(4)]).astype(np.float32)
    y_prompt = np.concatenate([np.ascontiguousarray(r[c]["yT_out"].T).reshape(4, 256, DM) for c in range(4, 8)]).astype(np.float32)
    nk = np.zeros((16, DEPTH, 256, 4, 2, 64), np.float32)
    nv = np.zeros((16, DEPTH, 256, 4, 128), np.float32)
    for c in range(4, 8):
        kt = r[c]["kT_out"]
        vv = r[c]["v_out"]
        for l in range(DEPTH):
            kk = np.ascontiguousarray(kt[l].T).reshape(4, 256, 4, 2, 64)
            nk[4 * (c - 4):4 * (c - 4) + 4, l] = kk
            nv[4 * (c - 4):4 * (c - 4) + 4, l] = vv[l].reshape(4, 256, 4, 128)
    return (y_prompt, y_sample, nk, nv)
```
